# Optimizing a Trainium2 kernel written in Bass

```python
import math
import jax, jax.numpy as jnp
from jax import lax
import numpy as np

D_MODEL = 4096
BATCH = 2
SEQ = 8192
DEPTH = 1

GRID_W = 64
CTX_LEN = 256

DIFF_HEADS = 8
DIFF_QK_DIM = 128
DIFF_V_DIM = 2 * DIFF_QK_DIM
DIFF_QK_W = DIFF_HEADS * 2 * DIFF_QK_DIM
DIFF_WIDTH = DIFF_HEADS * DIFF_V_DIM

RET_HEADS = 8
RET_K_DIM = 128
RET_V_DIM = 256
RET_QK_W = RET_HEADS * RET_K_DIM
RET_WIDTH = RET_HEADS * RET_V_DIM
RET_CHUNK = 128
RET_DECAY_BASE = 5.0

D_MIX = DIFF_WIDTH + RET_WIDTH

O_DQ = 0
O_DK = O_DQ + DIFF_QK_W
O_DV = O_DK + DIFF_QK_W
O_DG = O_DV + DIFF_WIDTH
O_RQ = O_DG + DIFF_WIDTH
O_RK = O_RQ + RET_QK_W
O_RV = O_RK + RET_QK_W
O_RG = O_RV + RET_WIDTH
D_IN_PROJ = O_RG + RET_WIDTH

ROPE_DIM = 128
ROPE_BASE = 10000.0
Q_BLOCK = 128
EPS = 1e-6

kernel_name = "hybrid_diffattn_retention_dit_layer"


def rms_norm(x, w):
    xf = x.astype(jnp.float32)
    y = xf * lax.rsqrt(jnp.mean(xf * xf, axis=-1, keepdims=True) + EPS)
    return (y * w.astype(jnp.float32)).astype(x.dtype)


def axial_rope_tables(n_tokens):
    rows = n_tokens // GRID_W
    row, col = jnp.meshgrid(jnp.arange(rows), jnp.arange(GRID_W), indexing="ij")
    row = row.reshape(-1).astype(jnp.float32)
    col = col.reshape(-1).astype(jnp.float32)
    half = ROPE_DIM // 2
    inv_freq = ROPE_BASE ** (-jnp.arange(0, half, 2, dtype=jnp.float32) / half)
    ang_r = row[:, None] * inv_freq
    ang_c = col[:, None] * inv_freq
    ang = jnp.concatenate([ang_r, ang_r, ang_c, ang_c], axis=-1)
    return jnp.cos(ang), jnp.sin(ang)


def apply_axial_rope(x, cos, sin):
    a, b, c_, d = jnp.split(x, 4, axis=-1)
    rot = jnp.concatenate([-b, a, -d, c_], axis=-1)
    cos = cos[None, :, None, :].astype(x.dtype)
    sin = sin[None, :, None, :].astype(x.dtype)
    return x * cos + rot * sin


def diff_attention(q, k, v, lam):
    B, L, H2, d = q.shape
    H = H2 // 2
    Lk = k.shape[1]
    nb = L // Q_BLOCK
    scale = d ** -0.5
    qb = q.reshape(B, nb, Q_BLOCK, H, 2, d).transpose(1, 0, 3, 4, 2, 5)
    kt = k.reshape(B, Lk, H, 2, d).transpose(0, 2, 3, 1, 4)
    vt = v.transpose(0, 2, 1, 3)

    def block(q_blk):
        s = jnp.einsum("bhiqd,bhikd->bhiqk", q_blk, kt).astype(jnp.float32) * scale
        p = jax.nn.softmax(s, axis=-1)
        a = p[:, :, 0] - lam * p[:, :, 1]
        return jnp.einsum("bhqk,bhkv->bqhv", a.astype(vt.dtype), vt)

    o = lax.map(block, qb)
    return o.transpose(1, 0, 2, 3, 4).reshape(B, L, H, v.shape[-1])


def log_decay(p):
    return jnp.log1p(-jnp.exp2(-p.astype(jnp.float32)))


def retention_chunked(q, k, v, log_gamma, r0, include_diag):
    B, H, L, dk = q.shape
    dv = v.shape[-1]
    C = RET_CHUNK
    nc = L // C
    qc = q.reshape(B, H, nc, C, dk)
    kc = k.reshape(B, H, nc, C, dk)
    vc = v.reshape(B, H, nc, C, dv)
    n = jnp.arange(C, dtype=jnp.float32)
    dist = n[:, None] - n[None, :]
    mask = dist >= 0 if include_diag else dist > 0
    decay = jnp.where(mask[None], jnp.exp(log_gamma[:, None, None] * jnp.where(mask, dist, 0.0)[None]), 0.0)
    scores = jnp.einsum("bhnqd,bhnkd->bhnqk", qc, kc) * decay[None, :, None]
    o_intra = jnp.einsum("bhnqk,bhnkv->bhnqv", scores, vc)
    xi = jnp.exp(log_gamma[:, None] * (n + 1.0))
    zeta = jnp.exp(log_gamma[:, None] * (C - 1.0 - n))
    chunk_kv = jnp.einsum("bhnkd,bhnkv->bhndv", kc * zeta[None, :, None, :, None], vc)
    gamma_c = jnp.exp(log_gamma * C)[None, :, None, None]

    def step(r, kv):
        return gamma_c * r + kv, r

    _, r_prev = lax.scan(step, r0, jnp.moveaxis(chunk_kv, 2, 0))
    r_prev = jnp.moveaxis(r_prev, 0, 2)
    o_cross = jnp.einsum("bhnqd,bhndv->bhnqv", qc, r_prev) * xi[None, :, None, :, None]
    return (o_intra + o_cross).reshape(B, H, L, dv)


def hybrid_layer(x, ctx, c, c_ctx, cos, sin, lam_init, norm_w, ada_w, ada_b, w_in,
                 q_norm_w, k_norm_w, lq1, lk1, lq2, lk2, subln_w,
                 dec_f, dec_b, ret_norm_w, w_out):
    B, L, D = x.shape
    Lc = ctx.shape[1]
    f32 = jnp.float32

    mod = jax.nn.silu(c) @ ada_w + ada_b
    shift, scale, gate = jnp.split(mod, 3, axis=-1)
    mod_c = jax.nn.silu(c_ctx) @ ada_w[:, :2 * D] + ada_b[:2 * D]
    shift_c, scale_c = jnp.split(mod_c, 2, axis=-1)
    h = rms_norm(x, norm_w) * (1.0 + scale[:, None]) + shift[:, None]
    hc = rms_norm(ctx, norm_w) * (1.0 + scale_c) + shift_c

    proj = h @ w_in
    dq, dk, dv, dg, rq, rk, rv, rg = jnp.split(proj, [O_DK, O_DV, O_DG, O_RQ, O_RK, O_RV, O_RG], axis=-1)
    w_ctx = jnp.concatenate([w_in[:, O_DK:O_DG], w_in[:, O_RK:O_RG]], axis=-1)
    pc = hc @ w_ctx
    c_dk, c_dv, c_rk, c_rv = jnp.split(pc, [DIFF_QK_W, DIFF_QK_W + DIFF_WIDTH, DIFF_QK_W + DIFF_WIDTH + RET_QK_W], axis=-1)

    dq = apply_axial_rope(rms_norm(dq.reshape(B, L, 2 * DIFF_HEADS, DIFF_QK_DIM), q_norm_w), cos, sin)
    dk = apply_axial_rope(rms_norm(dk.reshape(B, L, 2 * DIFF_HEADS, DIFF_QK_DIM), k_norm_w), cos, sin)
    c_dk = rms_norm(c_dk.reshape(B, Lc, 2 * DIFF_HEADS, DIFF_QK_DIM), k_norm_w)
    k_all = jnp.concatenate([dk, c_dk], axis=1)
    v_all = jnp.concatenate([dv.reshape(B, L, DIFF_HEADS, DIFF_V_DIM),
                             c_dv.reshape(B, Lc, DIFF_HEADS, DIFF_V_DIM)], axis=1)
    lam = (jnp.exp(jnp.sum(lq1.astype(f32) * lk1.astype(f32)))
           - jnp.exp(jnp.sum(lq2.astype(f32) * lk2.astype(f32))) + lam_init)
    o_d = diff_attention(dq, k_all, v_all, lam)
    o_d = rms_norm(o_d, subln_w) * (1.0 - lam_init)
    o_d = o_d.reshape(B, L, DIFF_WIDTH) * jax.nn.silu(dg)

    k_scale = RET_K_DIM ** -0.5
    rq = apply_axial_rope(rq.reshape(B, L, RET_HEADS, RET_K_DIM), cos, sin)
    rk = apply_axial_rope(rk.reshape(B, L, RET_HEADS, RET_K_DIM), cos, sin) * k_scale
    c_rk = c_rk.reshape(B, Lc, RET_HEADS, RET_K_DIM) * k_scale
    rv = rv.reshape(B, L, RET_HEADS, RET_V_DIM)
    c_rv = c_rv.reshape(B, Lc, RET_HEADS, RET_V_DIM)
    to_bhld = lambda t: t.astype(f32).transpose(0, 2, 1, 3)
    q_, k_, v_ = to_bhld(rq), to_bhld(rk), to_bhld(rv)
    kc_, vc_ = to_bhld(c_rk), to_bhld(c_rv)
    lg_f = log_decay(dec_f)
    lg_b = log_decay(dec_b)
    m = jnp.arange(Lc, dtype=f32)
    r_ctx_f = jnp.einsum("bhmd,bhmv->bhdv", kc_ * jnp.exp(lg_f[:, None] * (Lc - 1.0 - m))[None, :, :, None], vc_)
    r_ctx_b = jnp.einsum("bhmd,bhmv->bhdv", kc_ * jnp.exp(lg_b[:, None] * m)[None, :, :, None], vc_)
    o_f = retention_chunked(q_, k_, v_, lg_f, r_ctx_f, True)
    o_b = retention_chunked(q_[:, :, ::-1], k_[:, :, ::-1], v_[:, :, ::-1], lg_b, r_ctx_b, False)[:, :, ::-1]
    o_r = (o_f + o_b).transpose(0, 2, 1, 3)
    o_r = rms_norm(o_r, ret_norm_w).astype(x.dtype).reshape(B, L, RET_WIDTH) * jax.nn.silu(rg)

    y = jnp.concatenate([o_d, o_r], axis=-1) @ w_out
    return x + gate[:, None] * y


def setup_inputs(seed: int = 0) -> dict:
    key = jax.random.key(seed)
    ks = jax.random.split(key, 20)
    f32 = jnp.float32
    D = D_MODEL

    def nrm(k, shape, s):
        return jax.random.normal(k, shape, f32) * s

    decay_sched = RET_DECAY_BASE + jnp.arange(RET_HEADS, dtype=f32)
    return {
        "x": nrm(ks[0], (BATCH, SEQ, D), 1.0),
        "c": nrm(ks[1], (BATCH, D), 1.0),
        "ctx": nrm(ks[2], (BATCH, CTX_LEN, D), 1.0),
        "c_ctx": nrm(ks[3], (D,), 1.0),
        "norm_w": 1.0 + nrm(ks[4], (DEPTH, D), 0.02),
        "ada_w": nrm(ks[5], (DEPTH, D, 3 * D), D ** -0.5),
        "ada_b": nrm(ks[6], (DEPTH, 3 * D), 0.02),
        "w_in": nrm(ks[7], (DEPTH, D, D_IN_PROJ), D ** -0.5),
        "diff_q_norm_w": 1.0 + nrm(ks[8], (DEPTH, DIFF_QK_DIM), 0.02),
        "diff_k_norm_w": 1.0 + nrm(ks[9], (DEPTH, DIFF_QK_DIM), 0.02),
        "diff_lambda_q1": nrm(ks[10], (DEPTH, DIFF_QK_DIM), 0.1),
        "diff_lambda_k1": nrm(ks[11], (DEPTH, DIFF_QK_DIM), 0.1),
        "diff_lambda_q2": nrm(ks[12], (DEPTH, DIFF_QK_DIM), 0.1),
        "diff_lambda_k2": nrm(ks[13], (DEPTH, DIFF_QK_DIM), 0.1),
        "diff_subln_w": 1.0 + nrm(ks[14], (DEPTH, DIFF_V_DIM), 0.02),
        "ret_decay_fwd": decay_sched[None] + nrm(ks[15], (DEPTH, RET_HEADS), 0.1),
        "ret_decay_bwd": decay_sched[None] + nrm(ks[16], (DEPTH, RET_HEADS), 0.1),
        "ret_norm_w": 1.0 + nrm(ks[17], (DEPTH, RET_V_DIM), 0.02),
        "w_out": nrm(ks[18], (DEPTH, D_MIX, D), D_MIX ** -0.5),
    }


def reference(x, c, ctx, c_ctx, norm_w, ada_w, ada_b, w_in, diff_q_norm_w, diff_k_norm_w,
              diff_lambda_q1, diff_lambda_k1, diff_lambda_q2, diff_lambda_k2, diff_subln_w,
              ret_decay_fwd, ret_decay_bwd, ret_norm_w, w_out):
    L = x.shape[1]
    cos, sin = axial_rope_tables(L)
    for layer in range(DEPTH):
        lam_init = 0.8 - 0.6 * math.exp(-0.3 * layer)
        x = hybrid_layer(x, ctx, c, c_ctx, cos, sin, lam_init,
                         norm_w[layer], ada_w[layer], ada_b[layer], w_in[layer],
                         diff_q_norm_w[layer], diff_k_norm_w[layer],
                         diff_lambda_q1[layer], diff_lambda_k1[layer],
                         diff_lambda_q2[layer], diff_lambda_k2[layer], diff_subln_w[layer],
                         ret_decay_fwd[layer], ret_decay_bwd[layer], ret_norm_w[layer], w_out[layer])
    return x
```

```python
import os
from contextlib import ExitStack

import numpy as np
import ml_dtypes

import concourse.bass as bass
import concourse.mybir as mybir
from concourse.bass_utils import run_bass_kernel_spmd

F32 = mybir.dt.float32
BF16 = mybir.dt.bfloat16
AF = mybir.ActivationFunctionType
ALU = mybir.AluOpType
AX = mybir.AxisListType

D = 4096
L = 8192
LC = 256
LK = L + LC
NKC = 32
TB = 512
NBLK = L // TB
EPS = 1e-6
NCOLS = 3584
NADA = 9216
LAM_INIT = 0.8 - 0.6 * 1.0
ATT_SCALE = 128.0 ** -0.5
SB_BYTES = 212480


class Op:
    __slots__ = ("eng", "fn", "waits", "signaled", "semval", "inc")

    def __init__(self, eng, fn, waits, inc=None):
        self.eng = eng
        self.fn = fn
        self.waits = waits
        self.signaled = False
        self.semval = None
        self.inc = inc


class DmaH:
    __slots__ = ("sem", "val")

    def __init__(self, sem, val):
        self.sem = sem
        self.val = val


class Slot:
    def __init__(self, sem):
        self.sem = sem
        self.count = 0


class Plan:
    ENGS = ("pe", "act", "dve", "pool", "sp")

    def __init__(self, nc, stack):
        self.nc = nc
        self.stack = stack
        self.streams = {e: [] for e in self.ENGS}
        self.keys = {}
        self.esem = {e: stack.enter_context(nc.semaphore("s_" + e)) for e in self.ENGS}
        self.slots = []
        self.pending = {e: [] for e in self.ENGS}
        self.nslot = 0

    def slot(self):
        self.nslot += 1
        s = Slot(self.stack.enter_context(self.nc.semaphore("d%d" % self.nslot)))
        self.slots.append(s)
        return s

    def _deps(self, handle, reads, writes, after):
        waits = list(after)
        for k in reads:
            st = self.keys.setdefault(k, {"w": [], "r": [], "war": []})
            waits += st["w"]
        for k in writes:
            st = self.keys.setdefault(k, {"w": [], "r": [], "war": []})
            if st["r"] or (k in reads):
                st["war"] = st["r"] + st["w"]
                st["w"] = []
                st["r"] = []
            waits += st["war"]
        for k in reads:
            if k not in writes:
                self.keys[k]["r"].append(handle)
        for k in writes:
            self.keys[k]["w"].append(handle)
        return waits

    def op(self, eng, fn, reads=(), writes=(), after=()):
        o = Op(eng, fn, None)
        waits = self._deps(o, reads, writes, after)
        waits += self.pending[eng]
        self.pending[eng] = []
        o.waits = waits
        for w in waits:
            if isinstance(w, Op):
                w.signaled = True
        self.streams[eng].append(o)
        return o

    def dma(self, q, out, in_, slot, reads=(), writes=(), after=()):
        slot.count += 16
        h = DmaH(slot.sem, slot.count)
        o = Op(q, lambda e: e.dma_start(out=out, in_=in_), None, inc=(slot.sem, 16))
        waits = self._deps(h, reads, writes, after)
        waits += self.pending[q]
        self.pending[q] = []
        o.waits = waits
        for w in waits:
            if isinstance(w, Op):
                w.signaled = True
        self.streams[q].append(o)
        return h

    def barrier(self):
        hs = []
        for e in self.ENGS:
            if self.streams[e]:
                last = self.streams[e][-1]
                if last.inc is None:
                    last.signaled = True
                    hs.append(last)
                else:
                    for o in reversed(self.streams[e]):
                        if o.inc is None:
                            o.signaled = True
                            hs.append(o)
                            break
        for s in self.slots:
            if s.count:
                hs.append(DmaH(s.sem, s.count))
        for e in self.ENGS:
            self.pending[e] = self.pending[e] + hs
        self.keys = {}

    def emit(self, block):
        for e in self.ENGS:
            n = 0
            for o in self.streams[e]:
                if o.inc is None and o.signaled:
                    n += 1
                    o.semval = n
        esem = self.esem

        def run(e, ops):
            seen = {}
            for o in ops:
                need = {}
                for w in o.waits:
                    if isinstance(w, Op):
                        sem, val = esem[w.eng], w.semval
                    else:
                        sem, val = w.sem, w.val
                    k = id(sem)
                    if seen.get(k, 0) >= val:
                        continue
                    if k not in need or need[k][1] < val:
                        need[k] = (sem, val)
                for k, (sem, val) in need.items():
                    e.wait_ge(sem, val)
                    seen[k] = val
                ins = o.fn(e)
                if o.inc is not None:
                    if o.inc[1] is None:
                        ins.then_inc(o.inc[0])
                    else:
                        ins.then_inc(o.inc[0], o.inc[1])
                elif o.signaled:
                    ins.then_inc(esem[o.eng], 1)

        st = self.streams
        final = [DmaH(s.sem, s.count) for s in self.slots if s.count]

        @block.tensor
        def _(e):
            run(e, st["pe"])

        @block.scalar
        def _(e):
            run(e, st["act"])

        @block.vector
        def _(e):
            run(e, st["dve"])

        @block.gpsimd
        def _(e):
            run(e, st["pool"])
            for h in final:
                e.wait_ge(h.sem, h.val)

        @block.sync
        def _(e):
            run(e, st["sp"])
            for h in final:
                e.wait_ge(h.sem, h.val)


class SBAlloc:
    def __init__(self, big):
        self.big = big
        self.off = 0

    def reset(self, off):
        self.off = off

    def take(self, shape, dt):
        n = 1
        for s in shape:
            n *= s
        nbytes = n * (4 if dt == F32 else 2)
        nbytes = (nbytes + 63) // 64 * 64
        off = self.off
        self.off += nbytes
        assert self.off <= SB_BYTES, ("SBUF overflow", self.off)
        ap = self.big[:, off // 2:(off + nbytes) // 2]
        if dt == F32:
            ap = ap.bitcast(F32)
        ap = ap[:, 0:n]
        if len(shape) == 2:
            names = "a b"
            return ap.rearrange("p (a b) -> p a b", a=shape[0])
        if len(shape) == 3:
            return ap.rearrange("p (a b c) -> p a b c", a=shape[0], b=shape[1])
        return ap


def build_program(stop_after=99, debug=(), b_blocks=None, d_qblocks=None, e_blocks=None, e_gather=True):
    if b_blocks is None:
        b_blocks = [NBLK] + list(range(NBLK))
    nc = bass.Bass("TRN2", target_bir_lowering=False)

    def din(name, shape, dt=F32):
        return nc.dram_tensor(name, list(shape), dt, kind="ExternalInput").ap()

    def dscr(name, shape, dt=BF16):
        kind = "ExternalOutput" if name in debug else "Internal"
        return nc.dram_tensor(name, list(shape), dt, kind=kind).ap()

    x = din("x", [L, D])
    ctx = din("ctx", [LC, D])
    xrT = din("xrT", [1024, L])
    cvec = din("cvec", [128, NKC, 2])
    normw = din("normw", [128, NKC])
    adaw = din("adaw", [D, NADA])
    adab = din("adab", [128, 72])
    win = din("win", [D, NCOLS])
    wout = din("wout", [D, 1024])
    qnw = din("qnw", [128])
    knw = din("knw", [128])
    lam4 = din("lam4", [4 * 128])
    subw = din("subw", [256])
    retw = din("retw", [256])
    dec = din("dec", [4])
    ident_d = din("ident", [128, 128], BF16)
    ropeq = din("ropeq", [L, 2, 128])
    rc_mat = din("rc_mat", [128, 4, 128])
    rc_row = din("rc_row", [128, 2, 128])
    rc_col = din("rc_col", [128, 8])
    outT = nc.dram_tensor("outT", [1024, L], F32, kind="ExternalOutput").ap()

    win_bf = dscr("win_bf", [7, 128, NKC, 512])
    qT_s = dscr("qT_s", [4, 128, L])
    kT_s = dscr("kT_s", [4, 128, LK])
    v_s = dscr("v_s", [LK, 512])
    g_s = dscr("g_s", [L, 512])
    rqT_s = dscr("rqT_s", [2, 128, L])
    rkT_s = dscr("rkT_s", [2, 128, L])
    rk_s = dscr("rk_s", [LK, 256])
    rv_s = dscr("rv_s", [LK, 512])
    rg_s = dscr("rg_s", [L, 512])
    oT_loc_t = [nc.dram_tensor("oT_loc%d" % i, [1024, TB], BF16,
                               kind="ExternalOutput" if "oT_loc" in debug else "Internal") for i in range(NBLK)]
    oT_all_t = [nc.dram_tensor("oT_all%d" % i, [4096, TB], BF16) for i in range(NBLK)]
    dbg_mod = dscr("dbg_mod", [128, 144], F32) if "dbg_mod" in debug else None

    with ExitStack() as stack:
        big_t = stack.enter_context(nc.sbuf_tensor("big", [128, SB_BYTES // 2], BF16))
        big = big_t[:]
        PP = [stack.enter_context(nc.psum_tensor("pp%d" % i, [128, 1024], F32)) for i in range(4)]
        P = Plan(nc, stack)
        block = stack.enter_context(nc.Block())
        sb = SBAlloc(big)

        def psf(i):
            return PP[i // 2][:, (i % 2) * 512:(i % 2 + 1) * 512]

        def psb(i):
            return psf(i).bitcast(BF16)

        def act(out, in_, func, r, w, bias=None, scale=None, accum=None, after=()):
            kw = {}
            if bias is not None:
                kw["bias"] = bias
            if scale is not None:
                kw["scale"] = scale
            if accum is not None:
                kw["accum_out"] = accum
            return P.op("act", lambda e: e.activation(out=out, in_=in_, func=func, **kw), r, w, after)

        def tt(eng, out, in0, in1, op, r, w, after=()):
            return P.op(eng, lambda e: e.tensor_tensor(out=out, in0=in0, in1=in1, op=op), r, w, after)

        def ts(eng, out, in0, s1, s2, op0, op1, r, w, after=()):
            if s2 is None:
                return P.op(eng, lambda e: e.tensor_scalar(out=out, in0=in0, scalar1=s1, scalar2=None, op0=op0), r, w, after)
            return P.op(eng, lambda e: e.tensor_scalar(out=out, in0=in0, scalar1=s1, scalar2=s2, op0=op0, op1=op1), r, w, after)

        def stt(out, in0, scalar, in1, op0, op1, r, w, after=()):
            return P.op("dve", lambda e: e.scalar_tensor_tensor(out=out, in0=in0, scalar=scalar, in1=in1, op0=op0, op1=op1), r, w, after)

        def cp(eng, out, in_, r, w, after=()):
            if eng == "act":
                return P.op("act", lambda e: e.activation(out=out, in_=in_, func=AF.Copy), r, w, after)
            return P.op(eng, lambda e: e.tensor_copy(out=out, in_=in_), r, w, after)

        def recip(out, in_, r, w, after=()):
            return P.op("dve", lambda e: e.reciprocal(out=out, in_=in_), r, w, after)

        def red(out, in_, r, w, after=()):
            return P.op("dve", lambda e: e.tensor_reduce(out=out, in_=in_, axis=AX.X, op=ALU.add), r, w, after)

        def mm(out, lhsT, rhs, start, stop, r, w, after=(), skip=False):
            if skip:
                return P.op("pe", lambda e: e.matmul(out, lhsT, rhs, start=start, stop=stop, skip_group_check=True), r, w, after)
            return P.op("pe", lambda e: e.matmul(out, lhsT, rhs, start=start, stop=stop), r, w, after)

        def tr(out, in_, r, w, after=()):
            return P.op("pe", lambda e: e.transpose(out, in_, ident), r + ["ident"], w, after)

        def memset(eng, ap, val, r, w, after=()):
            return P.op(eng, lambda e: e.memset(ap, val), r, w, after)

        def rstd_of(out, ss, n, r, w):
            act(out, ss, AF.Ln, r, w, bias=EPS, scale=1.0 / n)
            act(out, out, AF.Exp, w, w, scale=-0.5)

        ident = sb.take([1, 128], BF16)[:, 0, :]
        modsb = sb.take([72, 2], F32)
        a_lat = sb.take([1, NKC], F32)[:, 0, :]
        a_ctx = sb.take([1, NKC], F32)[:, 0, :]
        normw_t = sb.take([1, NKC], F32)[:, 0, :]
        adab_t = sb.take([1, 72], F32)[:, 0, :]
        qnw_r = sb.take([1, 128], F32)[:, 0, :]
        knw_r = sb.take([1, 128], F32)[:, 0, :]
        subw_r = sb.take([1, 256], F32)[:, 0, :]
        retw_r = sb.take([1, 256], F32)[:, 0, :]
        lam_t = sb.take([4, 128], F32)
        lam_s = sb.take([1, 8], F32)[:, 0, :]
        dec_t = sb.take([1, 4], F32)[:, 0, :]
        lg_t = sb.take([1, 4], F32)[:, 0, :]
        rcm = sb.take([4, 128], F32)
        rcr = sb.take([2, 128], F32)
        rcc = sb.take([1, 8], F32)[:, 0, :]
        DT = [sb.take([1, 128], F32)[:, 0, :] for _ in range(2)]
        XIF = [sb.take([1, 128], BF16)[:, 0, :] for _ in range(2)]
        XIB = [sb.take([1, 128], BF16)[:, 0, :] for _ in range(2)]
        rsm = sb.take([2, 8], F32)
        sc_t = sb.take([NKC, 2], BF16)
        cv_t = sb.take([NKC, 2], F32)
        const_end = sb.off
        assert const_end <= 13 * 1024, const_end

        sl_c = P.slot()
        const_h = []
        for (dst, src, key) in (
            (ident, ident_d[:, :], "ident"),
            (cv_t, cvec[:, :, :], "cv"),
            (normw_t, normw[:, :], "normw"),
            (adab_t, adab[:, :], "adab"),
            (qnw_r, qnw.partition_broadcast(128), "qnw"),
            (knw_r, knw.partition_broadcast(128), "knw"),
            (subw_r, subw.partition_broadcast(128), "subw"),
            (retw_r, retw.partition_broadcast(128), "retw"),
            (lam_t, lam4.partition_broadcast(128).rearrange("p (a b) -> p a b", a=4), "lam"),
            (dec_t, dec.partition_broadcast(128), "dec"),
            (rcm, rc_mat[:, :, :], "rcm"),
            (rcr, rc_row[:, :, :], "rcr"),
            (rcc, rc_col[:, :], "rcc"),
        ):
            const_h.append(P.dma("sp", dst, src, sl_c, [], [key]))
        for h_ in const_h:
            h_.val = sl_c.count

        act(sc_t, cv_t, AF.Silu, ["cv"], ["sc"])

        pa0 = sb.off
        adaF = [sb.take([1, NADA], F32)[:, 0, :] for _ in range(2)]
        adaB = [sb.take([1, NADA], BF16)[:, 0, :] for _ in range(2)]
        wF = [sb.take([8, 512], F32) for _ in range(2)]
        wB = [sb.take([8, 512], BF16) for _ in range(2)]
        sl_ada = [P.slot(), P.slot()]
        sl_wf = [P.slot(), P.slot()]
        sl_wst = [P.slot(), P.slot()]
        pmod = psf(0)[:, 0:144]

        wpieces = [(cb, q) for cb in range(7) for q in range(4)]
        HALF = NADA // 2
        for kc in range(NKC):
            s = kc % 2
            P.dma("sp", adaF[s], adaw[kc * 128:(kc + 1) * 128, :], sl_ada[s], [], ["adaF%d" % s])
            cp("dve", adaB[s][:, 0:HALF], adaF[s][:, 0:HALF], ["adaF%d" % s], ["adaB%d" % s])
            cp("act", adaB[s][:, HALF:NADA], adaF[s][:, HALF:NADA], ["adaF%d" % s], ["adaB%d" % s])
            for j in range(72):
                mm(pmod[:, 2 * j:2 * j + 2], adaB[s][:, j * 128:(j + 1) * 128], sc_t[:, kc, :],
                   start=(kc == 0 and j == 0), stop=(kc == NKC - 1),
                   r=["adaB%d" % s, "sc"], w=["ps0"], skip=True)
            if kc < len(wpieces):
                cb, q = wpieces[kc]
                ws = kc % 2
                src = win[q * 1024:(q + 1) * 1024, cb * 512:(cb + 1) * 512].rearrange("(k p) n -> p k n", p=128)
                P.dma("sp", wF[ws], src, sl_wf[ws], [], ["wF%d" % ws])
                cp("pool", wB[ws], wF[ws], ["wF%d" % ws], ["wB%d" % ws])
                P.dma("pool", win_bf[cb, :, q * 8:(q + 1) * 8, :], wB[ws], sl_wst[ws], ["wB%d" % ws], ["win_bf"])
        tt("dve", modsb, pmod.rearrange("p (a b) -> p a b", b=2),
           adab_t.unsqueeze(2).to_broadcast([128, 72, 2]), ALU.add, ["ps0", "adab"], ["mod"])
        stt(a_lat, modsb[:, 32:64, 0], 1.0, normw_t, ALU.add, ALU.mult, ["mod", "normw"], ["a_lat"])
        stt(a_ctx, modsb[:, 32:64, 1], 1.0, normw_t, ALU.add, ALU.mult, ["mod", "normw"], ["a_ctx"])
        if dbg_mod is not None:
            P.dma("pool", dbg_mod[:, :], modsb.rearrange("p a b -> p (a b)"), P.slot(), ["mod"], [])

        junk128 = sb.take([1, 128], F32)[:, 0, :]
        tt("dve", junk128, lam_t[:, 0, :], lam_t[:, 1, :], ALU.mult, ["lam"], ["junk128"])
        red(lam_s[:, 2:3], junk128, ["junk128"], ["lam_s"])
        tt("dve", junk128, lam_t[:, 2, :], lam_t[:, 3, :], ALU.mult, ["lam", "lam_s"], ["junk128"])
        red(lam_s[:, 3:4], junk128, ["junk128"], ["lam_s"])
        act(lam_s[:, 4:6], lam_s[:, 2:4], AF.Exp, ["lam_s"], ["lam_s"])
        stt(lam_s[:, 0:1], lam_s[:, 4:5], LAM_INIT, lam_s[:, 5:6], ALU.add, ALU.subtract, ["lam_s"], ["lam_s"])
        ts("dve", lam_s[:, 1:2], lam_s[:, 0:1], -1.0, None, ALU.mult, None, ["lam_s"], ["lam_s"])
        ts("dve", subw_r, subw_r, 1.0 - LAM_INIT, None, ALU.mult, None, ["subw"], ["subw"])

        act(lg_t, dec_t, AF.Exp, ["dec"], ["lg"], scale=-float(np.log(2.0)))
        act(lg_t, lg_t, AF.Ln, ["lg"], ["lg"], bias=1.0, scale=-1.0)
        tmpm = sb.take([2, 128], F32)
        for hl in range(2):
            lf = lg_t[:, hl:hl + 1]
            lb = lg_t[:, 2 + hl:3 + hl]
            act(tmpm[:, 0, :], rcm[:, 0, :], AF.Exp, ["rcm", "lg"], ["tmpm"], scale=lf)
            act(tmpm[:, 1, :], rcm[:, 2, :], AF.Exp, ["rcm", "lg"], ["tmpm"], scale=lb)
            tt("dve", tmpm[:, 0, :], tmpm[:, 0, :], rcm[:, 1, :], ALU.mult, ["tmpm", "rcm"], ["tmpm"])
            tt("dve", tmpm[:, 1, :], tmpm[:, 1, :], rcm[:, 3, :], ALU.mult, ["tmpm", "rcm"], ["tmpm"])
            tt("dve", tmpm[:, 0, :], tmpm[:, 0, :], tmpm[:, 1, :], ALU.add, ["tmpm"], ["tmpm"])
            ts("dve", DT[hl], tmpm[:, 0, :], ATT_SCALE, None, ALU.mult, None, ["tmpm"], ["DT%d" % hl])
            act(tmpm[:, 0, :], rcr[:, 0, :], AF.Exp, ["rcr", "lg", "tmpm"], ["tmpm"], scale=lf)
            act(tmpm[:, 1, :], rcr[:, 1, :], AF.Exp, ["rcr", "lg", "tmpm"], ["tmpm"], scale=lb)
            ts("dve", XIF[hl], tmpm[:, 0, :], ATT_SCALE, None, ALU.mult, None, ["tmpm"], ["XIF%d" % hl])
            ts("dve", XIB[hl], tmpm[:, 1, :], ATT_SCALE, None, ALU.mult, None, ["tmpm"], ["XIB%d" % hl])
            act(rsm[:, hl, 0:1], rcc[:, 0:1], AF.Exp, ["rcc", "lg"], ["rsm"], scale=lf)
            act(rsm[:, hl, 1:2], rcc[:, 1:2], AF.Exp, ["rcc", "lg"], ["rsm"], scale=lb)
            act(rsm[:, hl, 2:3], rcc[:, 2:3], AF.Exp, ["rcc", "lg"], ["rsm"], scale=lf)
            act(rsm[:, hl, 3:4], rcc[:, 2:3], AF.Exp, ["rcc", "lg"], ["rsm"], scale=lb)
            act(rsm[:, hl, 4:5], rcc[:, 3:4], AF.Exp, ["rcc", "lg"], ["rsm"], scale=lf)
            act(rsm[:, hl, 5:6], rcc[:, 0:1], AF.Exp, ["rcc", "lg"], ["rsm"], scale=lf)
            act(rsm[:, hl, 6:7], rcc[:, 1:2], AF.Exp, ["rcc", "lg"], ["rsm"], scale=lb)
            act(rsm[:, hl, 7:8], rcc[:, 4:5], AF.Exp, ["rcc", "lg"], ["rsm"], scale=lb)

        P.barrier()
        if stop_after >= 1:
            sb.reset(pa0)
            xt = sb.take([1, D], F32)[:, 0, :]
            xs2 = [sb.take([1, D], BF16)[:, 0, :] for _ in range(2)]
            hT = [sb.take([NKC, TB], BF16) for _ in range(2)]
            Wb = [sb.take([NKC, 512], BF16) for _ in range(2)]
            rq_t = sb.take([4, 256], F32)
            fset = [(sb.take([4, 128], F32), sb.take([4, 128], F32), sb.take([4, 128], F32), sb.take([1, 8], F32)[:, 0, :]) for _ in range(2)]
            qr = [sb.take([4, 128], BF16) for _ in range(2)]
            ssn2 = [sb.take([1, 8], F32)[:, 0, :] for _ in range(2)]
            st_qT = sb.take([4, TB], BF16)
            st_kT = sb.take([4, TB], BF16)
            st_rT = sb.take([4, TB], BF16)
            st_v = [sb.take([1, 512], BF16)[:, 0, :]] * 2
            st_g = [sb.take([1, 512], BF16)[:, 0, :]] * 2
            st_rv = [sb.take([1, 512], BF16)[:, 0, :]] * 2
            st_rg = [sb.take([1, 512], BF16)[:, 0, :]] * 2
            sl_x = P.slot()
            sl_w = [P.slot(), P.slot()]
            sl_rope = P.slot()
            sl_rope2 = P.slot()
            sl_st = {k: P.slot() for k in ("qT", "kT", "rT", "v0", "v1", "g0", "g1", "rv0", "rv1", "rg0", "rg1", "rk0", "rk1")}
            cnt = {"w": 0, "acc": 0, "tq": 0, "ev": 0, "qr": 0, "tok": 0}

            def rope(src, NM, cosT, sinT, out, r, w, fi):
                f1, f2, f3, ss4 = fset[fi]
                F1, F3 = "f1_%d" % fi, "f3_%d" % fi
                f1v = f1[:, 0:NM, :]
                f3v = f3[:, 0:NM, :]
                tt("dve", f1v, src, cosT.unsqueeze(1).to_broadcast([128, NM, 128]), ALU.mult, r, [F1])
                s5 = src.rearrange("p m (h e q) -> p m h e q", h=2, e=2)
                o5 = f3v.rearrange("p m (h e q) -> p m h e q", h=2, e=2)
                sn = sinT.rearrange("p (h e q) -> p h e q", h=2, e=2)
                tt("dve", o5[:, :, :, 0, :], s5[:, :, :, 1, :],
                   sn[:, :, 0, :].unsqueeze(1).to_broadcast([128, NM, 2, 32]), ALU.mult, r, [F3])
                tt("dve", o5[:, :, :, 1, :], s5[:, :, :, 0, :],
                   sn[:, :, 1, :].unsqueeze(1).to_broadcast([128, NM, 2, 32]), ALU.mult, r, [F3])
                tt("dve", out, f1v, f3v, ALU.add, [F1, F3], w)

            def norm_p1(blk, t, sl):
                is_ctx = blk == NBLK
                src = ctx[t * 128:(t + 1) * 128, :] if is_ctx else x[blk * TB + t * 128: blk * TB + (t + 1) * 128, :]
                xs = xs2[sl]
                ssn = ssn2[sl]
                P.dma("sp", xt, src, sl_x, [], ["xt"])
                act(xs, xt, AF.Square, ["xt"], ["xs%d" % sl, "ssn%d_0" % sl], accum=ssn[:, 0:1])
                rstd_of(ssn[:, 1:2], ssn[:, 0:1], D, ["ssn%d_0" % sl], ["ssn%d_1" % sl])
                ts("dve", xs, xt, ssn[:, 1:2], None, ALU.mult, None, ["xt", "ssn%d_1" % sl], ["xs%d" % sl])

            def norm_p2(blk, t, hs, sl):
                is_ctx = blk == NBLK
                col = 1 if is_ctx else 0
                a_v = a_ctx if is_ctx else a_lat
                xs = xs2[sl]
                banks = (0, 1, 6, 0)

                def trg(g):
                    bank = banks[g]
                    for j in range(8):
                        kc = g * 8 + j
                        tr(psb(bank)[:, j * 128:(j + 1) * 128], xs[:, kc * 128:(kc + 1) * 128], ["xs%d" % sl], ["ps%d" % bank])

                def evg(g):
                    bank = banks[g]
                    for j in range(8):
                        kc = g * 8 + j
                        o_ = hT[hs][:, kc, t * 128:(t + 1) * 128]
                        i_ = psb(bank)[:, j * 128:(j + 1) * 128]
                        a_c = a_v[:, kc:kc + 1]
                        s_c = modsb[:, kc, col:col + 1]
                        if g % 2 == 0:
                            ts("dve", o_, i_, a_c, s_c, ALU.mult, ALU.add, ["ps%d" % bank], ["hT%d" % hs])
                        else:
                            act(o_, i_, AF.Identity, ["ps%d" % bank], ["hT%d" % hs], bias=s_c, scale=a_c)
                trg(0)
                trg(1)
                trg(2)
                evg(0)
                trg(3)
                evg(1)
                evg(2)
                evg(3)

            def store_tok(name, dst_rows, src_ap, key):
                P.dma("pool", dst_rows, src_ap, sl_st[name], [key], [name + "_dram"])

            def post(blk, t, cb, bank, ntile):
                is_ctx = blk == NBLK
                ps = psf(bank)
                pk = "ps%d" % bank
                krow = (L + t * 128) if is_ctx else (blk * TB + t * 128)
                cosq = rq_t[:, t, 0:128]
                sinq = rq_t[:, t, 128:256]
                cnt["f"] = cnt.get("f", 0) + 1
                fi = cnt["f"] % 2
                f1, f2, f3, ss4 = fset[fi]
                F1, F2 = "f1_%d" % fi, "f2_%d" % fi
                SA, SB_ = "ss4a_%d" % fi, "ss4b_%d" % fi
                last = t == ntile - 1
                if cb in (0, 1):
                    w_r = qnw_r if cb == 0 else knw_r
                    stg = st_qT if cb == 0 else st_kT
                    stn = "qT" if cb == 0 else "kT"
                    act(f1.rearrange("p a b -> p (a b)"), ps, AF.Square, [pk], [F1])
                    red(ss4[:, 0:4], f1, [F1], [SA])
                    rstd_of(ss4[:, 4:8], ss4[:, 0:4], 128, [SA], [SB_])
                    for m in range(4):
                        stt(f2[:, m, :], ps[:, m * 128:(m + 1) * 128], ss4[:, 4 + m:5 + m], w_r, ALU.mult, ALU.mult,
                            [pk, SB_], [F2])
                    cnt["qr"] += 1
                    q_ = qr[cnt["qr"] % 2]
                    qk = "qr%d" % (cnt["qr"] % 2)
                    if is_ctx:
                        cp("dve", q_, f2, [F2], [qk])
                    else:
                        rope(f2, 4, cosq, sinq, q_, [F2, "ropeq"], [qk], fi)
                    cnt["tq"] += 1
                    tb_ = (5, 7)[cnt["tq"] % 2]
                    for m in range(4):
                        tr(psb(tb_)[:, m * 128:(m + 1) * 128], q_[:, m, :], [qk], ["ps%d" % tb_])
                    cp("act", stg[:, :, t * 128:(t + 1) * 128], psb(tb_)[:, 0:512].rearrange("p (m t) -> p m t", m=4),
                       ["ps%d" % tb_], ["st_" + stn])
                    if last:
                        if cb == 0:
                            P.dma("pool", qT_s[:, :, blk * TB:(blk + 1) * TB].rearrange("m p t -> p m t"), stg,
                                  sl_st[stn], ["st_" + stn], ["qT_dram"])
                        elif is_ctx:
                            P.dma("pool", kT_s[:, :, L:LK].rearrange("m p t -> p m t"), stg[:, :, 0:LC],
                                  sl_st[stn], ["st_" + stn], ["kT_dram"])
                        else:
                            P.dma("pool", kT_s[:, :, blk * TB:(blk + 1) * TB].rearrange("m p t -> p m t"), stg,
                                  sl_st[stn], ["st_" + stn], ["kT_dram"])
                elif cb in (2, 5):
                    cnt["tok"] += 1
                    s_ = cnt["tok"] % 2
                    s_ = 0
                    nm = ("v%d" if cb == 2 else "rv%d") % s_
                    buf = (st_v if cb == 2 else st_rv)[s_]
                    dst = (v_s if cb == 2 else rv_s)[krow:krow + 128, :]
                    cp("act", buf, ps, [pk], [nm])
                    store_tok(nm, dst, buf, nm)
                elif cb in (3, 6):
                    cnt["tok"] += 1
                    s_ = cnt["tok"] % 2
                    s_ = 0
                    nm = ("g%d" if cb == 3 else "rg%d") % s_
                    buf = (st_g if cb == 3 else st_rg)[s_]
                    dst = (g_s if cb == 3 else rg_s)[krow:krow + 128, :]
                    act(buf, ps, AF.Silu, [pk], [nm])
                    store_tok(nm, dst, buf, nm)
                else:
                    cnt["qr"] += 1
                    q_ = qr[cnt["qr"] % 2]
                    qk = "qr%d" % (cnt["qr"] % 2)
                    p3 = ps.rearrange("p (m d) -> p m d", d=128)
                    if is_ctx:
                        cp("dve", q_[:, 2:4, :], p3[:, 2:4, :], [pk], [qk])
                    else:
                        rope(p3, 4, cosq, sinq, q_, [pk, "ropeq"], [qk], fi)
                    P.dma("pool", rk_s[krow:krow + 128, :].rearrange("p (m d) -> p m d", d=128), q_[:, 2:4, :], sl_st["rk%d" % (cnt["qr"] % 2)], [qk], ["rk_dram"])
                    if not is_ctx:
                        cnt["tq"] += 1
                        tb_ = (5, 7)[cnt["tq"] % 2]
                        for m in range(4):
                            tr(psb(tb_)[:, m * 128:(m + 1) * 128], q_[:, m, :], [qk], ["ps%d" % tb_])
                        cp("act", st_rT[:, :, t * 128:(t + 1) * 128], psb(tb_)[:, 0:512].rearrange("p (m t) -> p m t", m=4),
                           ["ps%d" % tb_], ["st_rT"])
                        if last:
                            P.dma("pool", rqT_s[:, :, blk * TB:(blk + 1) * TB].rearrange("m p t -> p m t"), st_rT[:, 0:2, :],
                                  sl_st["rT"], ["st_rT"], ["rT_dram"])
                            P.dma("pool", rkT_s[:, :, blk * TB:(blk + 1) * TB].rearrange("m p t -> p m t"), st_rT[:, 2:4, :],
                                  sl_st["rT"], ["st_rT"], ["rT_dram"])

            def mm_block(blk, hs, nxt):
                is_ctx = blk == NBLK
                ntile = 2 if is_ctx else 4
                cbs = (1, 2, 4, 5) if is_ctx else range(7)
                if not is_ctx:
                    P.dma("sp", rq_t, ropeq[blk * TB:(blk + 1) * TB, :, :].rearrange("(t p) c d -> p t (c d)", p=128),
                          sl_rope, [], ["ropeq"])
                nt_next = 0 if nxt is None else (2 if nxt == NBLK else 4)
                pd = [0, 0]

                def norm_step():
                    if pd[1] < pd[0]:
                        norm_p2(nxt, pd[1], 1 - hs, pd[1] % 2)
                        pd[1] += 1
                    if pd[0] < nt_next:
                        norm_p1(nxt, pd[0], pd[0] % 2)
                        pd[0] += 1
                for ci, cb in enumerate(cbs):
                    norm_step()
                    cnt["w"] += 1
                    ws = cnt["w"] % 2
                    P.dma("sp", Wb[ws], win_bf[cb], sl_w[ws], ["win_bf"], ["W%d" % ws])
                    for t in range(ntile):
                        cnt["acc"] += 1
                        bank = 2 + cnt["acc"] % 3
                        for kc in range(NKC):
                            mm(psf(bank), hT[hs][:, kc, t * 128:(t + 1) * 128], Wb[ws][:, kc, :],
                               start=(kc == 0), stop=(kc == NKC - 1), r=["hT%d" % hs, "W%d" % ws], w=["ps%d" % bank])
                        post(blk, t, cb, bank, ntile)
                while pd[1] < nt_next:
                    norm_step()

            def ntiles(blk):
                return 2 if blk == NBLK else 4

            for t in range(ntiles(b_blocks[0])):
                norm_p1(b_blocks[0], t, t % 2)
                norm_p2(b_blocks[0], t, 0, t % 2)
            for i, blk in enumerate(b_blocks):
                nxt = b_blocks[i + 1] if i + 1 < len(b_blocks) else None
                mm_block(blk, i % 2, nxt)
            P.barrier()
        ag_h = {}

        def dma_group(q, pairs, slot, reads, writes, after=()):
            hs = [P.dma(q, o_, i_, slot, reads, writes, after) for (o_, i_) in pairs]
            for h_ in hs:
                h_.val = slot.count
            return hs

        def out_norm_store(src_ps, src_key, w_rep, gate_ap, gate_key, st_tile, st_key, col0, tmps, tq_cnt, tbanks=(6, 7)):
            r_ = tq_cnt % len(tmps)
            ssr, yf, yb = tmps[r_]
            act(yf, src_ps, AF.Square, [src_key], ["yf%d" % r_, "ssr0_%d" % r_], accum=ssr[:, 0:1])
            rstd_of(ssr[:, 1:2], ssr[:, 0:1], 256, ["ssr0_%d" % r_], ["ssr1_%d" % r_])
            stt(yf, src_ps, ssr[:, 1:2], w_rep, ALU.mult, ALU.mult, [src_key, "ssr1_%d" % r_], ["yf%d" % r_])
            tt("dve", yb, yf, gate_ap, ALU.mult, ["yf%d" % r_, gate_key], ["yb%d" % r_])
            tb_ = tbanks[tq_cnt % len(tbanks)]
            for dc in range(2):
                tr(psb(tb_)[:, dc * 128:(dc + 1) * 128], yb[:, dc * 128:(dc + 1) * 128], ["yb%d" % r_], ["ps%d" % tb_])
            cp("dve", st_tile[:, :, col0:col0 + 128], psb(tb_)[:, 0:256].rearrange("p (c t) -> p c t", c=2),
               ["ps%d" % tb_], [st_key])

        if stop_after >= 2:
            sb.reset(pa0)
            rk_a = sb.take([64, 128], BF16)
            rv_a = sb.take([64, 256], BF16)
            rqT_a = sb.take([64, 128], BF16)
            rkT_a = sb.take([64, 128], BF16)
            RbS = sb.take([64, 256], BF16)
            kz = sb.take([64, 128], BF16)
            ckv = sb.take([2, 384], BF16)
            ckz = sb.take([2, 128], BF16)
            Rf = sb.take([1, 256], F32)[:, 0, :]
            Rb = sb.take([1, 256], F32)[:, 0, :]
            Rf_bf = sb.take([1, 256], BF16)[:, 0, :]
            rg_b = [sb.take([4, 256], BF16) for _ in range(2)]
            qx = [sb.take([2, 512], BF16) for _ in range(2)]
            ATb = [sb.take([1, 128], BF16)[:, 0, :] for _ in range(2)]
            tmpsC = [(sb.take([1, 8], F32)[:, 0, :], sb.take([1, 256], F32)[:, 0, :], sb.take([1, 256], BF16)[:, 0, :]) for _ in range(4)]
            st_oC = [sb.take([2, 512], BF16) for _ in range(2)]
            ssrC = sb.take([1, 8], F32)[:, 0, :]
            sl_c1 = [P.slot() for _ in range(5)]
            sl_rg = [P.slot(), P.slot()]
            sl_oC = [P.slot(), P.slot()]
            cC = {"kv": 0, "s": 0, "po": 0, "tq": 0, "at": 0}
            for hl in range(2):
                kcol = slice(hl * 128, (hl + 1) * 128)
                vcol = slice(hl * 256, (hl + 1) * 256)
                dma_group("sp", [(rk_a[:, q * 16:(q + 1) * 16, :],
                                  rk_s[q * 2048:(q + 1) * 2048, kcol].rearrange("(c p) d -> p c d", p=128)) for q in range(4)],
                          sl_c1[0], [], ["rk_a"])
                dma_group("sp", [(rv_a[:, q * 16:(q + 1) * 16, :],
                                  rv_s[q * 2048:(q + 1) * 2048, vcol].rearrange("(c p) d -> p c d", p=128)) for q in range(4)],
                          sl_c1[1], [], ["rv_a"])
                P.dma("sp", rqT_a.rearrange("p c t -> p (c t)"), rqT_s[hl], sl_c1[2], [], ["rqT_a"])
                P.dma("sp", rkT_a.rearrange("p c t -> p (c t)"), rkT_s[hl], sl_c1[3], [], ["rkT_a"])
                dma_group("sp", [(ckv[:, :, 0:128], rk_s[L:LK, kcol].rearrange("(c p) d -> p c d", p=128)),
                                 (ckv[:, :, 128:384], rv_s[L:LK, vcol].rearrange("(c p) d -> p c d", p=128))],
                          sl_c1[4], [], ["ckv"])
                for (dirn, Rst, rkey, w0) in (("f", Rf, "Rf", 4), ("b", Rb, "Rb", 6)):
                    for t in range(2):
                        ts("dve", ckz[:, t, :], ckv[:, t, 0:128], rsm[:, hl, w0 + t:w0 + t + 1], None, ALU.mult, None,
                           ["ckv"], ["ckz"])
                    cC["kv"] += 1
                    bk = 4 + cC["kv"] % 2
                    for t in range(2):
                        mm(psf(bk)[:, 0:256], ckz[:, t, :], ckv[:, t, 128:384], start=(t == 0), stop=(t == 1),
                           r=["ckz", "ckv"], w=["ps%d" % bk])
                    cp("dve", Rst, psf(bk)[:, 0:256], ["ps%d" % bk], [rkey])
                ts("dve", kz, rk_a, rsm[:, hl, 1:2], None, ALU.mult, None, ["rk_a"], ["kz"])
                for c in range(63, -1, -1):
                    cp("act", RbS[:, c, :], Rb, ["Rb"], ["RbS"])
                    cC["kv"] += 1
                    bk = 4 + cC["kv"] % 2
                    mm(psf(bk)[:, 0:256], kz[:, c, :], rv_a[:, c, :], start=True, stop=True, r=["kz", "rv_a"], w=["ps%d" % bk])
                    stt(Rb, Rb, rsm[:, hl, 3:4], psf(bk)[:, 0:256], ALU.mult, ALU.add, ["Rb", "ps%d" % bk], ["Rb"])
                ts("dve", kz, rk_a, rsm[:, hl, 0:1], None, ALU.mult, None, ["rk_a"], ["kz"])
                for gi in range(16):
                    s = gi % 2
                    P.dma("sp", rg_b[s], rg_s[gi * 512:(gi + 1) * 512, vcol].rearrange("(t p) d -> p t d", p=128),
                          sl_rg[s], [], ["rg%d" % s])
                    q4 = rqT_a[:, 4 * gi:4 * gi + 4, :]
                    tt("dve", qx[s][:, 0, :].rearrange("p (c t) -> p c t", c=4), q4,
                       XIF[hl].unsqueeze(1).to_broadcast([128, 4, 128]), ALU.mult, ["rqT_a"], ["qx%d" % s])
                    tt("dve", qx[s][:, 1, :].rearrange("p (c t) -> p c t", c=4), q4,
                       XIB[hl].unsqueeze(1).to_broadcast([128, 4, 128]), ALU.mult, ["rqT_a"], ["qx%d" % s])
                    for ci in range(4):
                        c = 4 * gi + ci
                        cC["s"] += 1
                        bs = cC["s"] % 2
                        mm(psf(bs)[:, 0:128], rkT_a[:, c, :], rqT_a[:, c, :], start=True, stop=True,
                           r=["rkT_a", "rqT_a"], w=["ps%d" % bs])
                        cC["at"] += 1
                        at = ATb[cC["at"] % 2]
                        atk = "AT%d" % (cC["at"] % 2)
                        tt("dve", at, psf(bs)[:, 0:128], DT[hl], ALU.mult, ["ps%d" % bs], [atk])
                        cp("act", Rf_bf, Rf, ["Rf"], ["Rf_bf"])
                        cC["po"] += 1
                        bp = 2 + cC["po"] % 2
                        pk = "ps%d" % bp
                        mm(psf(bp)[:, 0:256], at, rv_a[:, c, :], start=True, stop=False, r=[atk, "rv_a"], w=[pk])
                        mm(psf(bp)[:, 0:256], qx[s][:, 0, ci * 128:(ci + 1) * 128], Rf_bf, start=False, stop=False,
                           r=["qx%d" % s, "Rf_bf"], w=[pk])
                        mm(psf(bp)[:, 0:256], qx[s][:, 1, ci * 128:(ci + 1) * 128], RbS[:, c, :], start=False, stop=True,
                           r=["qx%d" % s, "RbS"], w=[pk])
                        cC["kv"] += 1
                        bk = 4 + cC["kv"] % 2
                        mm(psf(bk)[:, 0:256], kz[:, c, :], rv_a[:, c, :], start=True, stop=True, r=["kz", "rv_a"], w=["ps%d" % bk])
                        stt(Rf, Rf, rsm[:, hl, 2:3], psf(bk)[:, 0:256], ALU.mult, ALU.add, ["Rf", "ps%d" % bk], ["Rf"])
                        cC["tq"] += 1
                        out_norm_store(psf(bp)[:, 0:256], pk, retw_r, rg_b[s][:, ci, :], "rg%d" % s,
                                       st_oC[s], "st_oC%d" % s, ci * 128, tmpsC, cC["tq"])
                    r0 = 512 + hl * 256
                    P.dma("pool", oT_loc_t[gi].ap()[r0:r0 + 256, :].rearrange("(c p) t -> p c t", p=128),
                          st_oC[s], sl_oC[s], ["st_oC%d" % s], ["oT_loc"])
            P.barrier()

        if stop_after >= 3:
            sb.reset(pa0)
            wo_bf = sb.take([NKC, 1024], BF16)
            wF_e = sb.take([4, 1024], F32)
            e_base = sb.off
            sl_we = P.slot()
            NKT = LK // 128
            kT_h = sb.take([2, LK], BF16)
            V1 = sb.take([NKT, 258], BF16)
            qT_b = [sb.take([2, 512], BF16) for _ in range(2)]
            G_b = [sb.take([4, 256], BF16) for _ in range(2)]
            PT = [sb.take([1, 1024], BF16)[:, 0, :] for _ in range(3)]
            o1 = sb.take([4, 256], F32)
            rsD = sb.take([1, 8], F32)[:, 0, :]
            ssrD = sb.take([1, 8], F32)[:, 0, :]
            tmpsD = [(sb.take([1, 8], F32)[:, 0, :], sb.take([1, 256], F32)[:, 0, :], sb.take([1, 256], BF16)[:, 0, :]) for _ in range(4)]
            st_oD = [sb.take([2, 512], BF16) for _ in range(2)]
            sl_k = P.slot()
            sl_v = P.slot()
            sl_q = [P.slot(), P.slot()]
            sl_g = [P.slot(), P.slot()]
            sl_oD = [P.slot(), P.slot()]
            cD = {"tq": 0, "pt": 0, "s": 0}
            st_by_blk = {}
            memset("dve", V1[:, :, 256:258], 1.0, [], ["V1ones"])
            for hl in range(2):
                if hl == 1:
                    for q in range(8):
                        P.dma("sp", wF_e, wout[q * 512:(q + 1) * 512, :].rearrange("(k p) n -> p k n", p=128), sl_we, [], ["wF_e"])
                        cp(("dve", "pool")[q % 2], wo_bf[:, q * 4:(q + 1) * 4, :], wF_e, ["wF_e"], ["wo_bf"])
                dma_group("sp", [(kT_h[:, i, :], kT_s[2 * hl + i]) for i in range(2)], sl_k, [], ["kT_h"])
                dma_group("sp", [(V1[:, q * 11:(q + 1) * 11, 0:256],
                                  v_s[q * 1408:(q + 1) * 1408, hl * 256:(hl + 1) * 256].rearrange("(c p) d -> p c d", p=128))
                                 for q in range(6)], sl_v, [], ["V1"])
                for qb in (d_qblocks if d_qblocks is not None else range(16)):
                    s = qb % 2
                    P.dma("sp", qT_b[s], qT_s[2 * hl:2 * hl + 2, :, qb * 512:(qb + 1) * 512].rearrange("m p t -> p m t"),
                          sl_q[s], [], ["qT_b%d" % s])
                    P.dma("sp", G_b[s], g_s[qb * 512:(qb + 1) * 512, hl * 256:(hl + 1) * 256].rearrange("(t p) d -> p t d", p=128),
                          sl_g[s], [], ["G_b%d" % s])
                    for i in range(2):
                        def Sp(j):
                            pp = (0, 3)[j % 2]
                            for h in range(2):
                                kt = 2 * j + h
                                bs = 2 * pp + h
                                mm(psf(bs), kT_h[:, i, kt * 128:(kt + 1) * 128], qT_b[s][:, i, :], start=True, stop=True,
                                   r=["kT_h", "qT_b%d" % s], w=["ps%d" % bs])
                        NPR = NKT // 2
                        Sp(0)
                        for j in range(NPR):
                            if j + 1 < NPR:
                                Sp(j + 1)
                            pp = (0, 3)[j % 2]
                            cD["pt"] += 1
                            pr_ = cD["pt"] % 3
                            act(PT[pr_], PP[pp][:], AF.Exp, ["ps%d" % (2 * pp), "ps%d" % (2 * pp + 1)], ["PT%d" % pr_], scale=ATT_SCALE)
                            for h in range(2):
                                kt = 2 * j + h
                                for sub in range(4):
                                    mm(psf(2 + sub)[:, 0:257], PT[pr_][:, h * 512 + sub * 128:h * 512 + (sub + 1) * 128], V1[:, kt, 0:257],
                                       start=(kt == 0), stop=(kt == NKT - 1), r=["PT%d" % pr_, "V1", "V1ones"], w=["ps%d" % (2 + sub)])
                        for sub in range(4):
                            pk = "ps%d" % (2 + sub)
                            acc = psf(2 + sub)
                            recip(rsD[:, sub:sub + 1], acc[:, 256:257], [pk], ["rsD"])
                            if i == 0:
                                ts("dve", o1[:, sub, :], acc[:, 0:256], rsD[:, sub:sub + 1], None, ALU.mult, None, [pk, "rsD"], ["o1"])
                            else:
                                ts("dve", rsD[:, 4 + sub:5 + sub], rsD[:, sub:sub + 1], lam_s[:, 1:2], None, ALU.mult, None, ["rsD"], ["rsD2"])
                                stt(o1[:, sub, :], acc[:, 0:256], rsD[:, 4 + sub:5 + sub], o1[:, sub, :], ALU.mult, ALU.add,
                                    [pk, "rsD2", "o1"], ["o1"])
                    for sub in range(4):
                        cD["tq"] += 1
                        out_norm_store(o1[:, sub, :], "o1", subw_r, G_b[s][:, sub, :], "G_b%d" % s,
                                       st_oD[s], "st_oD%d" % s, sub * 128, tmpsD, cD["tq"], tbanks=(2 + sub,))
                    r0 = hl * 256
                    h_st = P.dma("pool", oT_loc_t[qb].ap()[r0:r0 + 256, :].rearrange("(c p) t -> p c t", p=128),
                                 st_oD[s], sl_oD[s], ["st_oD%d" % s], ["oT_loc"])
                    st_by_blk.setdefault(qb, []).append(h_st)
            P.barrier()

        if stop_after >= 4:
            sb.reset(e_base)
            oT_b = [sb.take([NKC, 512], BF16) for _ in range(2)]
            xr_b = [sb.take([8, 512], F32) for _ in range(2)]
            sl_ob = [P.slot(), P.slot()]
            sl_xr = [P.slot(), P.slot()]
            sl_out = [P.slot(), P.slot()]
            if e_gather:
                for qb in (e_blocks if e_blocks is not None else range(16)):
                    sl_ag = P.slot()
                    sl_ag.count = 1

                    def ag_fn(e, qb=qb):
                        return e.collective_compute(
                            "AllGather", ALU.bypass, replica_groups=[[0, 1, 2, 3], [4, 5, 6, 7]],
                            ins=[oT_loc_t[qb].ap().opt()], outs=[oT_all_t[qb].ap().opt()])
                    ag = Op("pool", ag_fn, list(P.pending["pool"]), inc=(sl_ag.sem, None))
                    P.pending["pool"] = []
                    P.streams["pool"].append(ag)
                    ag_h[qb] = DmaH(sl_ag.sem, 1)
            cE = {"acc": 0}
            for tb in (e_blocks if e_blocks is not None else range(16)):
                s = tb % 2
                cols = slice(tb * 512, (tb + 1) * 512)
                if e_gather:
                    dma_group("sp", [(oT_b[s][:, q * 16:(q + 1) * 16, :],
                                      oT_all_t[tb].ap()[q * 2048:(q + 1) * 2048, :].rearrange("(k p) t -> p k t", p=128)) for q in range(2)],
                              sl_ob[s], [], ["oT_b%d" % s], after=[ag_h[tb]])
                else:
                    dma_group("sp", [(oT_b[s][:, q * 8:(q + 1) * 8, :],
                                      oT_loc_t[tb].ap()[:, :].rearrange("(k p) t -> p k t", p=128)) for q in range(4)],
                              sl_ob[s], [], ["oT_b%d" % s])
                P.dma("sp", xr_b[s], xrT[:, cols].rearrange("(c p) t -> p c t", p=128), sl_xr[s], [], ["xr_b%d" % s])
                for cc in range(8):
                    cE["acc"] += 1
                    bk = cE["acc"] % 4
                    for kc in range(NKC):
                        mm(psf(bk), wo_bf[:, kc, cc * 128:(cc + 1) * 128], oT_b[s][:, kc, :], start=(kc == 0), stop=(kc == NKC - 1),
                           r=["wo_bf", "oT_b%d" % s], w=["ps%d" % bk])
                    stt(xr_b[s][:, cc, :], psf(bk), modsb[:, 64 + cc, 0:1], xr_b[s][:, cc, :], ALU.mult, ALU.add,
                        ["ps%d" % bk, "xr_b%d" % s], ["xr_b%d" % s])
                P.dma("pool", outT[:, cols].rearrange("(c p) t -> p c t", p=128), xr_b[s], sl_out[s], ["xr_b%d" % s], ["outT"])
        P.emit(block)
    return nc


O_DQ, O_DK, O_DV, O_DG, O_RQ, O_RK, O_RV, O_RG = 0, 2048, 4096, 6144, 8192, 9216, 10240, 12288


def _win_cols(g):
    hA, hB = 2 * g, 2 * g + 1
    cols = []
    for o in (O_DQ, O_DK, O_DV, O_DG):
        for h in (hA, hB):
            cols += list(range(o + h * 256, o + (h + 1) * 256))
    for o in (O_RQ, O_RK):
        for h in (hA, hB):
            cols += list(range(o + h * 128, o + (h + 1) * 128))
    for o in (O_RV, O_RG):
        for h in (hA, hB):
            cols += list(range(o + h * 256, o + (h + 1) * 256))
    return np.array(cols)


def _mix_rows():
    rows = []
    for r in range(4):
        for h in (2 * r, 2 * r + 1):
            rows += list(range(h * 256, (h + 1) * 256))
        for h in (2 * r, 2 * r + 1):
            rows += list(range(2048 + h * 256, 2048 + (h + 1) * 256))
    return np.array(rows)


def _const_tables():
    rows = L // 64
    row, col = np.meshgrid(np.arange(rows), np.arange(64), indexing="ij")
    row = row.reshape(-1).astype(np.float32)
    col = col.reshape(-1).astype(np.float32)
    half = 64
    inv_freq = (np.float32(10000.0) ** (-np.arange(0, half, 2, dtype=np.float32) / np.float32(half))).astype(np.float32)
    ang_r = row[:, None] * inv_freq
    ang_c = col[:, None] * inv_freq
    ang = np.concatenate([ang_r, ang_r, ang_c, ang_c], axis=-1).astype(np.float32)
    cos = np.cos(ang).astype(np.float32)
    sin = np.sin(ang).astype(np.float32)
    sign = np.concatenate([-np.ones(32), np.ones(32), -np.ones(32), np.ones(32)]).astype(np.float32)
    ropeq = np.stack([cos, sin * sign], axis=1).astype(np.float32)
    j = np.arange(128, dtype=np.float32)[:, None]
    i = np.arange(128, dtype=np.float32)[None, :]
    rc_mat = np.stack([np.maximum(i - j, 0), (i >= j).astype(np.float32),
                       np.maximum(j - i, 0), (j > i).astype(np.float32)], axis=1).astype(np.float32)
    rc_row = np.stack([np.broadcast_to(i + 1, (128, 128)), np.broadcast_to(128 - i, (128, 128))], axis=1).astype(np.float32)
    p = np.arange(128, dtype=np.float32)
    rc_col = np.zeros((128, 8), np.float32)
    rc_col[:, 0] = 127 - p
    rc_col[:, 1] = p
    rc_col[:, 2] = 128
    rc_col[:, 3] = 255 - p
    rc_col[:, 4] = 128 + p
    ident = np.eye(128, dtype=np.float32).astype(ml_dtypes.bfloat16)
    return dict(ropeq=np.ascontiguousarray(ropeq), rc_mat=np.ascontiguousarray(rc_mat),
                rc_row=np.ascontiguousarray(rc_row), rc_col=rc_col, ident=ident)


def prepare_inputs(x, c, ctx, c_ctx, norm_w, ada_w, ada_b, w_in, diff_q_norm_w, diff_k_norm_w,
                   diff_lambda_q1, diff_lambda_k1, diff_lambda_q2, diff_lambda_k2, diff_subln_w,
                   ret_decay_fwd, ret_decay_bwd, ret_norm_w, w_out):
    f = lambda a: np.asarray(a, dtype=np.float32)
    x, c, ctx, c_ctx = f(x), f(c), f(ctx), f(c_ctx)
    ada_w0, ada_b0, w_in0, w_out0 = f(ada_w)[0], f(ada_b)[0], f(w_in)[0], f(w_out)[0]
    consts = _const_tables()
    mix_rows = _mix_rows()

    def pk(v):
        return np.ascontiguousarray(v.reshape(-1, 128).T)

    normw = pk(f(norm_w)[0])
    lam4 = np.concatenate([f(diff_lambda_q1)[0], f(diff_lambda_k1)[0], f(diff_lambda_q2)[0], f(diff_lambda_k2)[0]])
    maps = []
    ada_ss = ada_w0[:, 0:8192]
    for i in range(8):
        b, g = i // 4, i % 4
        gcols = slice(8192 + g * 1024, 8192 + (g + 1) * 1024)
        m = dict(consts)
        m["x"] = x[b]
        m["ctx"] = ctx[b]
        m["xrT"] = np.ascontiguousarray(x[b][:, g * 1024:(g + 1) * 1024].T)
        m["cvec"] = np.ascontiguousarray(np.stack([pk(c[b]), pk(c_ctx)], axis=-1))
        m["normw"] = normw
        m["adaw"] = np.ascontiguousarray(np.concatenate([ada_ss, ada_w0[:, gcols]], axis=1))
        m["adab"] = pk(np.concatenate([ada_b0[0:8192], ada_b0[gcols]]))
        m["win"] = np.ascontiguousarray(w_in0[:, _win_cols(g)])
        m["wout"] = np.ascontiguousarray(w_out0[mix_rows][:, g * 1024:(g + 1) * 1024])
        m["qnw"] = f(diff_q_norm_w)[0]
        m["knw"] = f(diff_k_norm_w)[0]
        m["lam4"] = lam4
        m["subw"] = f(diff_subln_w)[0]
        m["retw"] = f(ret_norm_w)[0]
        m["dec"] = np.array([f(ret_decay_fwd)[0][2 * g], f(ret_decay_fwd)[0][2 * g + 1],
                             f(ret_decay_bwd)[0][2 * g], f(ret_decay_bwd)[0][2 * g + 1]], np.float32)
        maps.append(m)
    return maps


_NC_CACHE = {}


def kernel(**inputs):
    maps = prepare_inputs(**inputs)
    if "nc" not in _NC_CACHE:
        _NC_CACHE["nc"] = build_program()
    res = run_bass_kernel_spmd(_NC_CACHE["nc"], maps, core_ids=list(range(8)))
    out = np.empty((2, L, D), np.float32)
    for i in range(8):
        b, g = i // 4, i % 4
        out[b][:, g * 1024:(g + 1) * 1024] = res.results[i]["outT"].T
    return out
```

```python
import os
from contextlib import ExitStack

import numpy as np
import ml_dtypes

import concourse.bass as bass
import concourse.mybir as mybir
from concourse.bass_utils import run_bass_kernel_spmd

F32 = mybir.dt.float32
BF16 = mybir.dt.bfloat16
AF = mybir.ActivationFunctionType
ALU = mybir.AluOpType
AX = mybir.AxisListType

D = 4096
L = 8192
LC = 256
LK = L + LC
NKC = 32
TB = 512
NBLK = L // TB
EPS = 1e-6
NCOLS = 3584
NADA = 9216
LAM_INIT = 0.8 - 0.6 * 1.0
ATT_SCALE = 128.0 ** -0.5
SB_BYTES = 212480


class Op:
    __slots__ = ("eng", "fn", "waits", "signaled", "semval", "inc")

    def __init__(self, eng, fn, waits, inc=None):
        self.eng = eng
        self.fn = fn
        self.waits = waits
        self.signaled = False
        self.semval = None
        self.inc = inc


class DmaH:
    __slots__ = ("sem", "val")

    def __init__(self, sem, val):
        self.sem = sem
        self.val = val


class Slot:
    def __init__(self, sem):
        self.sem = sem
        self.count = 0


class Plan:
    ENGS = ("pe", "act", "dve", "pool", "sp")

    def __init__(self, nc, stack):
        self.nc = nc
        self.stack = stack
        self.streams = {e: [] for e in self.ENGS}
        self.keys = {}
        self.esem = {e: stack.enter_context(nc.semaphore("s_" + e)) for e in self.ENGS}
        self.slots = []
        self.pending = {e: [] for e in self.ENGS}
        self.nslot = 0

    def slot(self):
        self.nslot += 1
        s = Slot(self.stack.enter_context(self.nc.semaphore("d%d" % self.nslot)))
        self.slots.append(s)
        return s

    def _deps(self, handle, reads, writes, after):
        waits = list(after)
        for k in reads:
            st = self.keys.setdefault(k, {"w": [], "r": [], "war": []})
            waits += st["w"]
        for k in writes:
            st = self.keys.setdefault(k, {"w": [], "r": [], "war": []})
            if st["r"] or (k in reads):
                st["war"] = st["r"] + st["w"]
                st["w"] = []
                st["r"] = []
            waits += st["war"]
        for k in reads:
            if k not in writes:
                self.keys[k]["r"].append(handle)
        for k in writes:
            self.keys[k]["w"].append(handle)
        return waits

    def op(self, eng, fn, reads=(), writes=(), after=()):
        o = Op(eng, fn, None)
        waits = self._deps(o, reads, writes, after)
        waits += self.pending[eng]
        self.pending[eng] = []
        o.waits = waits
        for w in waits:
            if isinstance(w, Op):
                w.signaled = True
        self.streams[eng].append(o)
        return o

    def dma(self, q, out, in_, slot, reads=(), writes=(), after=()):
        slot.count += 16
        h = DmaH(slot.sem, slot.count)
        o = Op(q, lambda e: e.dma_start(out=out, in_=in_), None, inc=(slot.sem, 16))
        waits = self._deps(h, reads, writes, after)
        waits += self.pending[q]
        self.pending[q] = []
        o.waits = waits
        for w in waits:
            if isinstance(w, Op):
                w.signaled = True
        self.streams[q].append(o)
        return h

    def barrier(self):
        hs = []
        for e in self.ENGS:
            if self.streams[e]:
                last = self.streams[e][-1]
                if last.inc is None:
                    last.signaled = True
                    hs.append(last)
                else:
                    for o in reversed(self.streams[e]):
                        if o.inc is None:
                            o.signaled = True
                            hs.append(o)
                            break
        for s in self.slots:
            if s.count:
                hs.append(DmaH(s.sem, s.count))
        for e in self.ENGS:
            self.pending[e] = self.pending[e] + hs
        self.keys = {}

    def emit(self, block):
        for e in self.ENGS:
            n = 0
            for o in self.streams[e]:
                if o.inc is None and o.signaled:
                    n += 1
                    o.semval = n
        esem = self.esem

        def run(e, ops):
            seen = {}
            for o in ops:
                need = {}
                for w in o.waits:
                    if isinstance(w, Op):
                        sem, val = esem[w.eng], w.semval
                    else:
                        sem, val = w.sem, w.val
                    k = id(sem)
                    if seen.get(k, 0) >= val:
                        continue
                    if k not in need or need[k][1] < val:
                        need[k] = (sem, val)
                for k, (sem, val) in need.items():
                    e.wait_ge(sem, val)
                    seen[k] = val
                ins = o.fn(e)
                if o.inc is not None:
                    if o.inc[1] is None:
                        ins.then_inc(o.inc[0])
                    else:
                        ins.then_inc(o.inc[0], o.inc[1])
                elif o.signaled:
                    ins.then_inc(esem[o.eng], 1)

        st = self.streams
        final = [DmaH(s.sem, s.count) for s in self.slots if s.count]

        @block.tensor
        def _(e):
            run(e, st["pe"])

        @block.scalar
        def _(e):
            run(e, st["act"])

        @block.vector
        def _(e):
            run(e, st["dve"])

        @block.gpsimd
        def _(e):
            run(e, st["pool"])
            for h in final:
                e.wait_ge(h.sem, h.val)

        @block.sync
        def _(e):
            run(e, st["sp"])
            for h in final:
                e.wait_ge(h.sem, h.val)


class SBAlloc:
    def __init__(self, big):
        self.big = big
        self.off = 0

    def reset(self, off):
        self.off = off

    def take(self, shape, dt):
        n = 1
        for s in shape:
            n *= s
        nbytes = n * (4 if dt == F32 else 2)
        nbytes = (nbytes + 63) // 64 * 64
        off = self.off
        self.off += nbytes
        assert self.off <= SB_BYTES, ("SBUF overflow", self.off)
        ap = self.big[:, off // 2:(off + nbytes) // 2]
        if dt == F32:
            ap = ap.bitcast(F32)
        ap = ap[:, 0:n]
        if len(shape) == 2:
            names = "a b"
            return ap.rearrange("p (a b) -> p a b", a=shape[0])
        if len(shape) == 3:
            return ap.rearrange("p (a b c) -> p a b c", a=shape[0], b=shape[1])
        return ap


def build_program(stop_after=99, debug=(), b_blocks=None, d_qblocks=None, e_blocks=None, e_gather=True):
    if b_blocks is None:
        b_blocks = [NBLK] + list(range(NBLK))
    nc = bass.Bass("TRN2", target_bir_lowering=False)

    def din(name, shape, dt=F32):
        return nc.dram_tensor(name, list(shape), dt, kind="ExternalInput").ap()

    def dscr(name, shape, dt=BF16):
        kind = "ExternalOutput" if name in debug else "Internal"
        return nc.dram_tensor(name, list(shape), dt, kind=kind).ap()

    x = din("x", [L, D])
    ctx = din("ctx", [LC, D])
    xrT = din("xrT", [1024, L])
    cvec = din("cvec", [128, NKC, 2])
    normw = din("normw", [128, NKC])
    adaw = din("adaw", [D, NADA])
    adab = din("adab", [128, 72])
    win = din("win", [D, NCOLS])
    wout = din("wout", [D, 1024])
    qnw = din("qnw", [128])
    knw = din("knw", [128])
    lam4 = din("lam4", [4 * 128])
    subw = din("subw", [256])
    retw = din("retw", [256])
    dec = din("dec", [4])
    ident_d = din("ident", [128, 128], BF16)
    ropeq = din("ropeq", [L, 2, 128])
    rc_mat = din("rc_mat", [128, 4, 128])
    rc_row = din("rc_row", [128, 2, 128])
    rc_col = din("rc_col", [128, 8])
    outT = nc.dram_tensor("outT", [1024, L], F32, kind="ExternalOutput").ap()

    win_bf = dscr("win_bf", [7, 128, NKC, 512])
    qT_s = dscr("qT_s", [4, 128, L])
    kT_s = dscr("kT_s", [4, 128, LK])
    v_s = dscr("v_s", [LK, 512])
    g_s = dscr("g_s", [L, 512])
    rqT_s = dscr("rqT_s", [2, 128, L])
    rkT_s = dscr("rkT_s", [2, 128, L])
    rk_s = dscr("rk_s", [LK, 256])
    rv_s = dscr("rv_s", [LK, 512])
    rg_s = dscr("rg_s", [L, 512])
    oT_loc_t = [nc.dram_tensor("oT_loc%d" % i, [1024, TB], BF16,
                               kind="ExternalOutput" if "oT_loc" in debug else "Internal") for i in range(NBLK)]
    oT_all_t = [nc.dram_tensor("oT_all%d" % i, [4096, TB], BF16) for i in range(NBLK)]
    dbg_mod = dscr("dbg_mod", [128, 144], F32) if "dbg_mod" in debug else None

    with ExitStack() as stack:
        big_t = stack.enter_context(nc.sbuf_tensor("big", [128, SB_BYTES // 2], BF16))
        big = big_t[:]
        PP = [stack.enter_context(nc.psum_tensor("pp%d" % i, [128, 1024], F32)) for i in range(4)]
        P = Plan(nc, stack)
        block = stack.enter_context(nc.Block())
        sb = SBAlloc(big)

        def psf(i):
            return PP[i // 2][:, (i % 2) * 512:(i % 2 + 1) * 512]

        def psb(i):
            return psf(i).bitcast(BF16)

        def act(out, in_, func, r, w, bias=None, scale=None, accum=None, after=()):
            kw = {}
            if bias is not None:
                kw["bias"] = bias
            if scale is not None:
                kw["scale"] = scale
            if accum is not None:
                kw["accum_out"] = accum
            return P.op("act", lambda e: e.activation(out=out, in_=in_, func=func, **kw), r, w, after)

        def tt(eng, out, in0, in1, op, r, w, after=()):
            return P.op(eng, lambda e: e.tensor_tensor(out=out, in0=in0, in1=in1, op=op), r, w, after)

        def ts(eng, out, in0, s1, s2, op0, op1, r, w, after=()):
            if s2 is None:
                return P.op(eng, lambda e: e.tensor_scalar(out=out, in0=in0, scalar1=s1, scalar2=None, op0=op0), r, w, after)
            return P.op(eng, lambda e: e.tensor_scalar(out=out, in0=in0, scalar1=s1, scalar2=s2, op0=op0, op1=op1), r, w, after)

        def stt(out, in0, scalar, in1, op0, op1, r, w, after=()):
            return P.op("dve", lambda e: e.scalar_tensor_tensor(out=out, in0=in0, scalar=scalar, in1=in1, op0=op0, op1=op1), r, w, after)

        def cp(eng, out, in_, r, w, after=()):
            if eng == "act":
                return P.op("act", lambda e: e.activation(out=out, in_=in_, func=AF.Copy), r, w, after)
            return P.op(eng, lambda e: e.tensor_copy(out=out, in_=in_), r, w, after)

        def recip(out, in_, r, w, after=()):
            return P.op("dve", lambda e: e.reciprocal(out=out, in_=in_), r, w, after)

        def red(out, in_, r, w, after=()):
            return P.op("dve", lambda e: e.tensor_reduce(out=out, in_=in_, axis=AX.X, op=ALU.add), r, w, after)

        def mm(out, lhsT, rhs, start, stop, r, w, after=(), skip=False):
            if skip:
                return P.op("pe", lambda e: e.matmul(out, lhsT, rhs, start=start, stop=stop, skip_group_check=True), r, w, after)
            return P.op("pe", lambda e: e.matmul(out, lhsT, rhs, start=start, stop=stop), r, w, after)

        def tr(out, in_, r, w, after=()):
            return P.op("pe", lambda e: e.transpose(out, in_, ident), r + ["ident"], w, after)

        def memset(eng, ap, val, r, w, after=()):
            return P.op(eng, lambda e: e.memset(ap, val), r, w, after)

        def rstd_of(out, ss, n, r, w):
            act(out, ss, AF.Ln, r, w, bias=EPS, scale=1.0 / n)
            act(out, out, AF.Exp, w, w, scale=-0.5)

        ident = sb.take([1, 128], BF16)[:, 0, :]
        modsb = sb.take([72, 2], F32)
        a_lat = sb.take([1, NKC], F32)[:, 0, :]
        a_ctx = sb.take([1, NKC], F32)[:, 0, :]
        normw_t = sb.take([1, NKC], F32)[:, 0, :]
        adab_t = sb.take([1, 72], F32)[:, 0, :]
        qnw_r = sb.take([1, 128], F32)[:, 0, :]
        knw_r = sb.take([1, 128], F32)[:, 0, :]
        subw_r = sb.take([1, 256], F32)[:, 0, :]
        retw_r = sb.take([1, 256], F32)[:, 0, :]
        lam_t = sb.take([4, 128], F32)
        lam_s = sb.take([1, 8], F32)[:, 0, :]
        dec_t = sb.take([1, 4], F32)[:, 0, :]
        lg_t = sb.take([1, 4], F32)[:, 0, :]
        rcm = sb.take([4, 128], F32)
        rcr = sb.take([2, 128], F32)
        rcc = sb.take([1, 8], F32)[:, 0, :]
        DT = [sb.take([1, 128], F32)[:, 0, :] for _ in range(2)]
        XIF = [sb.take([1, 128], BF16)[:, 0, :] for _ in range(2)]
        XIB = [sb.take([1, 128], BF16)[:, 0, :] for _ in range(2)]
        rsm = sb.take([2, 8], F32)
        sc_t = sb.take([NKC, 2], BF16)
        cv_t = sb.take([NKC, 2], F32)
        const_end = sb.off
        assert const_end <= 13 * 1024, const_end

        sl_c = P.slot()
        const_h = []
        for (dst, src, key) in (
            (ident, ident_d[:, :], "ident"),
            (cv_t, cvec[:, :, :], "cv"),
            (normw_t, normw[:, :], "normw"),
            (adab_t, adab[:, :], "adab"),
            (qnw_r, qnw.partition_broadcast(128), "qnw"),
            (knw_r, knw.partition_broadcast(128), "knw"),
            (subw_r, subw.partition_broadcast(128), "subw"),
            (retw_r, retw.partition_broadcast(128), "retw"),
            (lam_t, lam4.partition_broadcast(128).rearrange("p (a b) -> p a b", a=4), "lam"),
            (dec_t, dec.partition_broadcast(128), "dec"),
            (rcm, rc_mat[:, :, :], "rcm"),
            (rcr, rc_row[:, :, :], "rcr"),
            (rcc, rc_col[:, :], "rcc"),
        ):
            const_h.append(P.dma("sp", dst, src, sl_c, [], [key]))
        for h_ in const_h:
            h_.val = sl_c.count

        act(sc_t, cv_t, AF.Silu, ["cv"], ["sc"])

        pa0 = sb.off
        adaF = [sb.take([1, NADA], F32)[:, 0, :] for _ in range(2)]
        adaB = [sb.take([1, NADA], BF16)[:, 0, :] for _ in range(2)]
        wF = [sb.take([8, 512], F32) for _ in range(2)]
        wB = [sb.take([8, 512], BF16) for _ in range(2)]
        sl_ada = [P.slot(), P.slot()]
        sl_wf = [P.slot(), P.slot()]
        sl_wst = [P.slot(), P.slot()]
        pmod = psf(0)[:, 0:144]

        wpieces = [(cb, q) for cb in range(7) for q in range(4)]
        HALF = NADA // 2
        for kc in range(NKC):
            s = kc % 2
            P.dma("sp", adaF[s], adaw[kc * 128:(kc + 1) * 128, :], sl_ada[s], [], ["adaF%d" % s])
            cp("dve", adaB[s][:, 0:HALF], adaF[s][:, 0:HALF], ["adaF%d" % s], ["adaB%d" % s])
            cp("act", adaB[s][:, HALF:NADA], adaF[s][:, HALF:NADA], ["adaF%d" % s], ["adaB%d" % s])
            for j in range(72):
                mm(pmod[:, 2 * j:2 * j + 2], adaB[s][:, j * 128:(j + 1) * 128], sc_t[:, kc, :],
                   start=(kc == 0 and j == 0), stop=(kc == NKC - 1),
                   r=["adaB%d" % s, "sc"], w=["ps0"], skip=True)
            if kc < len(wpieces):
                cb, q = wpieces[kc]
                ws = kc % 2
                src = win[q * 1024:(q + 1) * 1024, cb * 512:(cb + 1) * 512].rearrange("(k p) n -> p k n", p=128)
                P.dma("sp", wF[ws], src, sl_wf[ws], [], ["wF%d" % ws])
                cp("pool", wB[ws], wF[ws], ["wF%d" % ws], ["wB%d" % ws])
                P.dma("pool", win_bf[cb, :, q * 8:(q + 1) * 8, :], wB[ws], sl_wst[ws], ["wB%d" % ws], ["win_bf"])
        tt("dve", modsb, pmod.rearrange("p (a b) -> p a b", b=2),
           adab_t.unsqueeze(2).to_broadcast([128, 72, 2]), ALU.add, ["ps0", "adab"], ["mod"])
        stt(a_lat, modsb[:, 32:64, 0], 1.0, normw_t, ALU.add, ALU.mult, ["mod", "normw"], ["a_lat"])
        stt(a_ctx, modsb[:, 32:64, 1], 1.0, normw_t, ALU.add, ALU.mult, ["mod", "normw"], ["a_ctx"])
        if dbg_mod is not None:
            P.dma("pool", dbg_mod[:, :], modsb.rearrange("p a b -> p (a b)"), P.slot(), ["mod"], [])

        junk128 = sb.take([1, 128], F32)[:, 0, :]
        tt("dve", junk128, lam_t[:, 0, :], lam_t[:, 1, :], ALU.mult, ["lam"], ["junk128"])
        red(lam_s[:, 2:3], junk128, ["junk128"], ["lam_s"])
        tt("dve", junk128, lam_t[:, 2, :], lam_t[:, 3, :], ALU.mult, ["lam", "lam_s"], ["junk128"])
        red(lam_s[:, 3:4], junk128, ["junk128"], ["lam_s"])
        act(lam_s[:, 4:6], lam_s[:, 2:4], AF.Exp, ["lam_s"], ["lam_s"])
        stt(lam_s[:, 0:1], lam_s[:, 4:5], LAM_INIT, lam_s[:, 5:6], ALU.add, ALU.subtract, ["lam_s"], ["lam_s"])
        ts("dve", lam_s[:, 1:2], lam_s[:, 0:1], -1.0, None, ALU.mult, None, ["lam_s"], ["lam_s"])
        ts("dve", subw_r, subw_r, 1.0 - LAM_INIT, None, ALU.mult, None, ["subw"], ["subw"])

        act(lg_t, dec_t, AF.Exp, ["dec"], ["lg"], scale=-float(np.log(2.0)))
        act(lg_t, lg_t, AF.Ln, ["lg"], ["lg"], bias=1.0, scale=-1.0)
        tmpm = sb.take([2, 128], F32)
        for hl in range(2):
            lf = lg_t[:, hl:hl + 1]
            lb = lg_t[:, 2 + hl:3 + hl]
            act(tmpm[:, 0, :], rcm[:, 0, :], AF.Exp, ["rcm", "lg"], ["tmpm"], scale=lf)
            act(tmpm[:, 1, :], rcm[:, 2, :], AF.Exp, ["rcm", "lg"], ["tmpm"], scale=lb)
            tt("dve", tmpm[:, 0, :], tmpm[:, 0, :], rcm[:, 1, :], ALU.mult, ["tmpm", "rcm"], ["tmpm"])
            tt("dve", tmpm[:, 1, :], tmpm[:, 1, :], rcm[:, 3, :], ALU.mult, ["tmpm", "rcm"], ["tmpm"])
            tt("dve", tmpm[:, 0, :], tmpm[:, 0, :], tmpm[:, 1, :], ALU.add, ["tmpm"], ["tmpm"])
            ts("dve", DT[hl], tmpm[:, 0, :], ATT_SCALE, None, ALU.mult, None, ["tmpm"], ["DT%d" % hl])
            act(tmpm[:, 0, :], rcr[:, 0, :], AF.Exp, ["rcr", "lg", "tmpm"], ["tmpm"], scale=lf)
            act(tmpm[:, 1, :], rcr[:, 1, :], AF.Exp, ["rcr", "lg", "tmpm"], ["tmpm"], scale=lb)
            ts("dve", XIF[hl], tmpm[:, 0, :], ATT_SCALE, None, ALU.mult, None, ["tmpm"], ["XIF%d" % hl])
            ts("dve", XIB[hl], tmpm[:, 1, :], ATT_SCALE, None, ALU.mult, None, ["tmpm"], ["XIB%d" % hl])
            act(rsm[:, hl, 0:1], rcc[:, 0:1], AF.Exp, ["rcc", "lg"], ["rsm"], scale=lf)
            act(rsm[:, hl, 1:2], rcc[:, 1:2], AF.Exp, ["rcc", "lg"], ["rsm"], scale=lb)
            act(rsm[:, hl, 2:3], rcc[:, 2:3], AF.Exp, ["rcc", "lg"], ["rsm"], scale=lf)
            act(rsm[:, hl, 3:4], rcc[:, 2:3], AF.Exp, ["rcc", "lg"], ["rsm"], scale=lb)
            act(rsm[:, hl, 4:5], rcc[:, 3:4], AF.Exp, ["rcc", "lg"], ["rsm"], scale=lf)
            act(rsm[:, hl, 5:6], rcc[:, 0:1], AF.Exp, ["rcc", "lg"], ["rsm"], scale=lf)
            act(rsm[:, hl, 6:7], rcc[:, 1:2], AF.Exp, ["rcc", "lg"], ["rsm"], scale=lb)
            act(rsm[:, hl, 7:8], rcc[:, 4:5], AF.Exp, ["rcc", "lg"], ["rsm"], scale=lb)

        P.barrier()
        if stop_after >= 1:
            sb.reset(pa0)
            xt = sb.take([1, D], F32)[:, 0, :]
            xs2 = [sb.take([1, D], BF16)[:, 0, :] for _ in range(2)]
            hT = [sb.take([NKC, TB], BF16) for _ in range(2)]
            Wb = [sb.take([NKC, 512], BF16) for _ in range(2)]
            rq_t = sb.take([4, 256], F32)
            fset = [(sb.take([4, 128], F32), sb.take([4, 128], F32), sb.take([4, 128], F32), sb.take([1, 8], F32)[:, 0, :]) for _ in range(2)]
            qr = [sb.take([4, 128], BF16) for _ in range(2)]
            ssn2 = [sb.take([1, 8], F32)[:, 0, :] for _ in range(2)]
            st_qT = sb.take([4, TB], BF16)
            st_kT = sb.take([4, TB], BF16)
            st_rT = sb.take([4, TB], BF16)
            st_v = [sb.take([1, 512], BF16)[:, 0, :]] * 2
            st_g = [sb.take([1, 512], BF16)[:, 0, :]] * 2
            st_rv = [sb.take([1, 512], BF16)[:, 0, :]] * 2
            st_rg = [sb.take([1, 512], BF16)[:, 0, :]] * 2
            sl_x = P.slot()
            sl_w = [P.slot(), P.slot()]
            sl_rope = P.slot()
            sl_rope2 = P.slot()
            sl_st = {k: P.slot() for k in ("qT", "kT", "rT", "v0", "v1", "g0", "g1", "rv0", "rv1", "rg0", "rg1", "rk0", "rk1")}
            cnt = {"w": 0, "acc": 0, "tq": 0, "ev": 0, "qr": 0, "tok": 0}
            pend = []

            def rope(src, NM, cosT, sinT, out, r, w, fi):
                f1, f2, f3, ss4 = fset[fi]
                F1, F3 = "f1_%d" % fi, "f3_%d" % fi
                f1v = f1[:, 0:NM, :]
                f3v = f3[:, 0:NM, :]
                tt("dve", f1v, src, cosT.unsqueeze(1).to_broadcast([128, NM, 128]), ALU.mult, r, [F1])
                s5 = src.rearrange("p m (h e q) -> p m h e q", h=2, e=2)
                o5 = f3v.rearrange("p m (h e q) -> p m h e q", h=2, e=2)
                sn = sinT.rearrange("p (h e q) -> p h e q", h=2, e=2)
                tt("dve", o5[:, :, :, 0, :], s5[:, :, :, 1, :],
                   sn[:, :, 0, :].unsqueeze(1).to_broadcast([128, NM, 2, 32]), ALU.mult, r, [F3])
                tt("dve", o5[:, :, :, 1, :], s5[:, :, :, 0, :],
                   sn[:, :, 1, :].unsqueeze(1).to_broadcast([128, NM, 2, 32]), ALU.mult, r, [F3])
                tt("dve", out, f1v, f3v, ALU.add, [F1, F3], w)

            def norm_p1(blk, t, sl):
                is_ctx = blk == NBLK
                src = ctx[t * 128:(t + 1) * 128, :] if is_ctx else x[blk * TB + t * 128: blk * TB + (t + 1) * 128, :]
                xs = xs2[sl]
                ssn = ssn2[sl]
                P.dma("sp", xt, src, sl_x, [], ["xt"])
                act(xs, xt, AF.Square, ["xt"], ["xs%d" % sl, "ssn%d_0" % sl], accum=ssn[:, 0:1])
                rstd_of(ssn[:, 1:2], ssn[:, 0:1], D, ["ssn%d_0" % sl], ["ssn%d_1" % sl])
                ts("dve", xs, xt, ssn[:, 1:2], None, ALU.mult, None, ["xt", "ssn%d_1" % sl], ["xs%d" % sl])

            def norm_p2(blk, t, hs, sl):
                is_ctx = blk == NBLK
                col = 1 if is_ctx else 0
                a_v = a_ctx if is_ctx else a_lat
                xs = xs2[sl]
                banks = (0, 1, 6, 0)

                def trg(g):
                    bank = banks[g]
                    for j in range(8):
                        kc = g * 8 + j
                        tr(psb(bank)[:, j * 128:(j + 1) * 128], xs[:, kc * 128:(kc + 1) * 128], ["xs%d" % sl], ["ps%d" % bank])

                def evg(g):
                    bank = banks[g]
                    for j in range(8):
                        kc = g * 8 + j
                        o_ = hT[hs][:, kc, t * 128:(t + 1) * 128]
                        i_ = psb(bank)[:, j * 128:(j + 1) * 128]
                        a_c = a_v[:, kc:kc + 1]
                        s_c = modsb[:, kc, col:col + 1]
                        if g % 2 == 0:
                            ts("dve", o_, i_, a_c, s_c, ALU.mult, ALU.add, ["ps%d" % bank], ["hT%d" % hs])
                        else:
                            act(o_, i_, AF.Identity, ["ps%d" % bank], ["hT%d" % hs], bias=s_c, scale=a_c)
                trg(0)
                trg(1)
                trg(2)
                evg(0)
                trg(3)
                evg(1)
                evg(2)
                evg(3)

            def store_tok(name, dst_rows, src_ap, key):
                P.dma("pool", dst_rows, src_ap, sl_st[name], [key], [name + "_dram"])

            def post(blk, t, cb, bank, ntile):
                is_ctx = blk == NBLK
                ps = psf(bank)
                pk = "ps%d" % bank
                krow = (L + t * 128) if is_ctx else (blk * TB + t * 128)
                cosq = rq_t[:, t, 0:128]
                sinq = rq_t[:, t, 128:256]
                cnt["f"] = cnt.get("f", 0) + 1
                fi = cnt["f"] % 2
                f1, f2, f3, ss4 = fset[fi]
                F1, F2 = "f1_%d" % fi, "f2_%d" % fi
                SA, SB_ = "ss4a_%d" % fi, "ss4b_%d" % fi
                last = t == ntile - 1
                if cb in (0, 1):
                    w_r = qnw_r if cb == 0 else knw_r
                    stg = st_qT if cb == 0 else st_kT
                    stn = "qT" if cb == 0 else "kT"
                    act(f1.rearrange("p a b -> p (a b)"), ps, AF.Square, [pk], [F1])
                    red(ss4[:, 0:4], f1, [F1], [SA])
                    rstd_of(ss4[:, 4:8], ss4[:, 0:4], 128, [SA], [SB_])
                    for m in range(4):
                        stt(f2[:, m, :], ps[:, m * 128:(m + 1) * 128], ss4[:, 4 + m:5 + m], w_r, ALU.mult, ALU.mult,
                            [pk, SB_], [F2])
                    cnt["qr"] += 1
                    q_ = qr[cnt["qr"] % 2]
                    qk = "qr%d" % (cnt["qr"] % 2)
                    if is_ctx:
                        cp("dve", q_, f2, [F2], [qk])
                    else:
                        rope(f2, 4, cosq, sinq, q_, [F2, "ropeq"], [qk], fi)
                    def pb(q_=q_, qk=qk, stg=stg, stn=stn, t=t, last=last, cb=cb, blk=blk, is_ctx=is_ctx):
                        cnt["tq"] += 1
                        tb_ = (5, 7)[cnt["tq"] % 2]
                        for m in range(4):
                            tr(psb(tb_)[:, m * 128:(m + 1) * 128], q_[:, m, :], [qk], ["ps%d" % tb_])
                        cp("act", stg[:, :, t * 128:(t + 1) * 128], psb(tb_)[:, 0:512].rearrange("p (m t) -> p m t", m=4),
                           ["ps%d" % tb_], ["st_" + stn])
                        if last:
                            if cb == 0:
                                P.dma("pool", qT_s[:, :, blk * TB:(blk + 1) * TB].rearrange("m p t -> p m t"), stg,
                                      sl_st[stn], ["st_" + stn], ["qT_dram"])
                            elif is_ctx:
                                P.dma("pool", kT_s[:, :, L:LK].rearrange("m p t -> p m t"), stg[:, :, 0:LC],
                                      sl_st[stn], ["st_" + stn], ["kT_dram"])
                            else:
                                P.dma("pool", kT_s[:, :, blk * TB:(blk + 1) * TB].rearrange("m p t -> p m t"), stg,
                                      sl_st[stn], ["st_" + stn], ["kT_dram"])
                    pend.append(pb)
                elif cb in (2, 5):
                    cnt["tok"] += 1
                    s_ = cnt["tok"] % 2
                    s_ = 0
                    nm = ("v%d" if cb == 2 else "rv%d") % s_
                    buf = (st_v if cb == 2 else st_rv)[s_]
                    dst = (v_s if cb == 2 else rv_s)[krow:krow + 128, :]
                    cp("act", buf, ps, [pk], [nm])
                    store_tok(nm, dst, buf, nm)
                elif cb in (3, 6):
                    cnt["tok"] += 1
                    s_ = cnt["tok"] % 2
                    s_ = 0
                    nm = ("g%d" if cb == 3 else "rg%d") % s_
                    buf = (st_g if cb == 3 else st_rg)[s_]
                    dst = (g_s if cb == 3 else rg_s)[krow:krow + 128, :]
                    act(buf, ps, AF.Silu, [pk], [nm])
                    store_tok(nm, dst, buf, nm)
                else:
                    cnt["qr"] += 1
                    q_ = qr[cnt["qr"] % 2]
                    qk = "qr%d" % (cnt["qr"] % 2)
                    p3 = ps.rearrange("p (m d) -> p m d", d=128)
                    if is_ctx:
                        cp("dve", q_[:, 2:4, :], p3[:, 2:4, :], [pk], [qk])
                    else:
                        rope(p3, 4, cosq, sinq, q_, [pk, "ropeq"], [qk], fi)
                    P.dma("pool", rk_s[krow:krow + 128, :].rearrange("p (m d) -> p m d", d=128), q_[:, 2:4, :], sl_st["rk%d" % (cnt["qr"] % 2)], [qk], ["rk_dram"])
                    if not is_ctx:
                        def pb(q_=q_, qk=qk, t=t, last=last, blk=blk):
                            cnt["tq"] += 1
                            tb_ = (5, 7)[cnt["tq"] % 2]
                            for m in range(4):
                                tr(psb(tb_)[:, m * 128:(m + 1) * 128], q_[:, m, :], [qk], ["ps%d" % tb_])
                            cp("act", st_rT[:, :, t * 128:(t + 1) * 128], psb(tb_)[:, 0:512].rearrange("p (m t) -> p m t", m=4),
                               ["ps%d" % tb_], ["st_rT"])
                            if last:
                                P.dma("pool", rqT_s[:, :, blk * TB:(blk + 1) * TB].rearrange("m p t -> p m t"), st_rT[:, 0:2, :],
                                      sl_st["rT"], ["st_rT"], ["rT_dram"])
                                P.dma("pool", rkT_s[:, :, blk * TB:(blk + 1) * TB].rearrange("m p t -> p m t"), st_rT[:, 2:4, :],
                                      sl_st["rT"], ["st_rT"], ["rT_dram"])
                        pend.append(pb)

            def mm_block(blk, hs, nxt):
                is_ctx = blk == NBLK
                ntile = 2 if is_ctx else 4
                cbs = (1, 2, 4, 5) if is_ctx else range(7)
                if not is_ctx:
                    P.dma("sp", rq_t, ropeq[blk * TB:(blk + 1) * TB, :, :].rearrange("(t p) c d -> p t (c d)", p=128),
                          sl_rope, [], ["ropeq"])
                nt_next = 0 if nxt is None else (2 if nxt == NBLK else 4)
                pd = [0, 0]

                def norm_step():
                    if pd[1] < pd[0]:
                        norm_p2(nxt, pd[1], 1 - hs, pd[1] % 2)
                        pd[1] += 1
                    if pd[0] < nt_next:
                        norm_p1(nxt, pd[0], pd[0] % 2)
                        pd[0] += 1
                for ci, cb in enumerate(cbs):
                    norm_step()
                    cnt["w"] += 1
                    ws = cnt["w"] % 2
                    P.dma("sp", Wb[ws], win_bf[cb], sl_w[ws], ["win_bf"], ["W%d" % ws])
                    for t in range(ntile):
                        cnt["acc"] += 1
                        bank = 2 + cnt["acc"] % 3
                        for kc in range(NKC):
                            mm(psf(bank), hT[hs][:, kc, t * 128:(t + 1) * 128], Wb[ws][:, kc, :],
                               start=(kc == 0), stop=(kc == NKC - 1), r=["hT%d" % hs, "W%d" % ws], w=["ps%d" % bank])
                        while pend:
                            pend.pop(0)()
                        post(blk, t, cb, bank, ntile)
                while pend:
                    pend.pop(0)()
                while pd[1] < nt_next:
                    norm_step()

            def ntiles(blk):
                return 2 if blk == NBLK else 4

            for t in range(ntiles(b_blocks[0])):
                norm_p1(b_blocks[0], t, t % 2)
                norm_p2(b_blocks[0], t, 0, t % 2)
            for i, blk in enumerate(b_blocks):
                nxt = b_blocks[i + 1] if i + 1 < len(b_blocks) else None
                mm_block(blk, i % 2, nxt)
            P.barrier()
        ag_h = {}

        def dma_group(q, pairs, slot, reads, writes, after=()):
            hs = [P.dma(q, o_, i_, slot, reads, writes, after) for (o_, i_) in pairs]
            for h_ in hs:
                h_.val = slot.count
            return hs

        def out_norm_store(src_ps, src_key, w_rep, gate_ap, gate_key, st_tile, st_key, col0, tmps, tq_cnt, tbanks=(6, 7)):
            r_ = tq_cnt % len(tmps)
            ssr, yf, yb = tmps[r_]
            act(yf, src_ps, AF.Square, [src_key], ["yf%d" % r_, "ssr0_%d" % r_], accum=ssr[:, 0:1])
            rstd_of(ssr[:, 1:2], ssr[:, 0:1], 256, ["ssr0_%d" % r_], ["ssr1_%d" % r_])
            stt(yf, src_ps, ssr[:, 1:2], w_rep, ALU.mult, ALU.mult, [src_key, "ssr1_%d" % r_], ["yf%d" % r_])
            tt("dve", yb, yf, gate_ap, ALU.mult, ["yf%d" % r_, gate_key], ["yb%d" % r_])
            tb_ = tbanks[tq_cnt % len(tbanks)]
            for dc in range(2):
                tr(psb(tb_)[:, dc * 128:(dc + 1) * 128], yb[:, dc * 128:(dc + 1) * 128], ["yb%d" % r_], ["ps%d" % tb_])
            cp("dve", st_tile[:, :, col0:col0 + 128], psb(tb_)[:, 0:256].rearrange("p (c t) -> p c t", c=2),
               ["ps%d" % tb_], [st_key])

        if stop_after >= 2:
            sb.reset(pa0)
            rk_a = sb.take([64, 128], BF16)
            rv_a = sb.take([64, 256], BF16)
            rqT_a = sb.take([64, 128], BF16)
            rkT_a = sb.take([64, 128], BF16)
            RbS = sb.take([64, 256], BF16)
            kz = sb.take([64, 128], BF16)
            ckv = sb.take([2, 384], BF16)
            ckz = sb.take([2, 128], BF16)
            Rf = sb.take([1, 256], F32)[:, 0, :]
            Rb = sb.take([1, 256], F32)[:, 0, :]
            Rf_bf = sb.take([1, 256], BF16)[:, 0, :]
            rg_b = [sb.take([4, 256], BF16) for _ in range(2)]
            qx = [sb.take([2, 512], BF16) for _ in range(2)]
            ATb = [sb.take([1, 128], BF16)[:, 0, :] for _ in range(2)]
            tmpsC = [(sb.take([1, 8], F32)[:, 0, :], sb.take([1, 256], F32)[:, 0, :], sb.take([1, 256], BF16)[:, 0, :]) for _ in range(4)]
            st_oC = [sb.take([2, 512], BF16) for _ in range(2)]
            ssrC = sb.take([1, 8], F32)[:, 0, :]
            sl_c1 = [P.slot() for _ in range(5)]
            sl_rg = [P.slot(), P.slot()]
            sl_oC = [P.slot(), P.slot()]
            cC = {"kv": 0, "s": 0, "po": 0, "tq": 0, "at": 0}
            for hl in range(2):
                kcol = slice(hl * 128, (hl + 1) * 128)
                vcol = slice(hl * 256, (hl + 1) * 256)
                dma_group("sp", [(rk_a[:, q * 16:(q + 1) * 16, :],
                                  rk_s[q * 2048:(q + 1) * 2048, kcol].rearrange("(c p) d -> p c d", p=128)) for q in range(4)],
                          sl_c1[0], [], ["rk_a"])
                dma_group("sp", [(rv_a[:, q * 16:(q + 1) * 16, :],
                                  rv_s[q * 2048:(q + 1) * 2048, vcol].rearrange("(c p) d -> p c d", p=128)) for q in range(4)],
                          sl_c1[1], [], ["rv_a"])
                P.dma("sp", rqT_a.rearrange("p c t -> p (c t)"), rqT_s[hl], sl_c1[2], [], ["rqT_a"])
                P.dma("sp", rkT_a.rearrange("p c t -> p (c t)"), rkT_s[hl], sl_c1[3], [], ["rkT_a"])
                dma_group("sp", [(ckv[:, :, 0:128], rk_s[L:LK, kcol].rearrange("(c p) d -> p c d", p=128)),
                                 (ckv[:, :, 128:384], rv_s[L:LK, vcol].rearrange("(c p) d -> p c d", p=128))],
                          sl_c1[4], [], ["ckv"])
                for (dirn, Rst, rkey, w0) in (("f", Rf, "Rf", 4), ("b", Rb, "Rb", 6)):
                    for t in range(2):
                        ts("dve", ckz[:, t, :], ckv[:, t, 0:128], rsm[:, hl, w0 + t:w0 + t + 1], None, ALU.mult, None,
                           ["ckv"], ["ckz"])
                    cC["kv"] += 1
                    bk = 4 + cC["kv"] % 2
                    for t in range(2):
                        mm(psf(bk)[:, 0:256], ckz[:, t, :], ckv[:, t, 128:384], start=(t == 0), stop=(t == 1),
                           r=["ckz", "ckv"], w=["ps%d" % bk])
                    cp("dve", Rst, psf(bk)[:, 0:256], ["ps%d" % bk], [rkey])
                ts("dve", kz, rk_a, rsm[:, hl, 1:2], None, ALU.mult, None, ["rk_a"], ["kz"])
                for c in range(63, -1, -1):
                    cp("act", RbS[:, c, :], Rb, ["Rb"], ["RbS"])
                    cC["kv"] += 1
                    bk = 4 + cC["kv"] % 2
                    mm(psf(bk)[:, 0:256], kz[:, c, :], rv_a[:, c, :], start=True, stop=True, r=["kz", "rv_a"], w=["ps%d" % bk])
                    stt(Rb, Rb, rsm[:, hl, 3:4], psf(bk)[:, 0:256], ALU.mult, ALU.add, ["Rb", "ps%d" % bk], ["Rb"])
                ts("dve", kz, rk_a, rsm[:, hl, 0:1], None, ALU.mult, None, ["rk_a"], ["kz"])
                for gi in range(16):
                    s = gi % 2
                    P.dma("sp", rg_b[s], rg_s[gi * 512:(gi + 1) * 512, vcol].rearrange("(t p) d -> p t d", p=128),
                          sl_rg[s], [], ["rg%d" % s])
                    q4 = rqT_a[:, 4 * gi:4 * gi + 4, :]
                    tt("dve", qx[s][:, 0, :].rearrange("p (c t) -> p c t", c=4), q4,
                       XIF[hl].unsqueeze(1).to_broadcast([128, 4, 128]), ALU.mult, ["rqT_a"], ["qx%d" % s])
                    tt("dve", qx[s][:, 1, :].rearrange("p (c t) -> p c t", c=4), q4,
                       XIB[hl].unsqueeze(1).to_broadcast([128, 4, 128]), ALU.mult, ["rqT_a"], ["qx%d" % s])
                    for ci in range(4):
                        c = 4 * gi + ci
                        cC["s"] += 1
                        bs = cC["s"] % 2
                        mm(psf(bs)[:, 0:128], rkT_a[:, c, :], rqT_a[:, c, :], start=True, stop=True,
                           r=["rkT_a", "rqT_a"], w=["ps%d" % bs])
                        cC["at"] += 1
                        at = ATb[cC["at"] % 2]
                        atk = "AT%d" % (cC["at"] % 2)
                        tt("dve", at, psf(bs)[:, 0:128], DT[hl], ALU.mult, ["ps%d" % bs], [atk])
                        cp("act", Rf_bf, Rf, ["Rf"], ["Rf_bf"])
                        cC["po"] += 1
                        bp = 2 + cC["po"] % 2
                        pk = "ps%d" % bp
                        mm(psf(bp)[:, 0:256], at, rv_a[:, c, :], start=True, stop=False, r=[atk, "rv_a"], w=[pk])
                        mm(psf(bp)[:, 0:256], qx[s][:, 0, ci * 128:(ci + 1) * 128], Rf_bf, start=False, stop=False,
                           r=["qx%d" % s, "Rf_bf"], w=[pk])
                        mm(psf(bp)[:, 0:256], qx[s][:, 1, ci * 128:(ci + 1) * 128], RbS[:, c, :], start=False, stop=True,
                           r=["qx%d" % s, "RbS"], w=[pk])
                        cC["kv"] += 1
                        bk = 4 + cC["kv"] % 2
                        mm(psf(bk)[:, 0:256], kz[:, c, :], rv_a[:, c, :], start=True, stop=True, r=["kz", "rv_a"], w=["ps%d" % bk])
                        stt(Rf, Rf, rsm[:, hl, 2:3], psf(bk)[:, 0:256], ALU.mult, ALU.add, ["Rf", "ps%d" % bk], ["Rf"])
                        cC["tq"] += 1
                        out_norm_store(psf(bp)[:, 0:256], pk, retw_r, rg_b[s][:, ci, :], "rg%d" % s,
                                       st_oC[s], "st_oC%d" % s, ci * 128, tmpsC, cC["tq"])
                    r0 = 512 + hl * 256
                    P.dma("pool", oT_loc_t[gi].ap()[r0:r0 + 256, :].rearrange("(c p) t -> p c t", p=128),
                          st_oC[s], sl_oC[s], ["st_oC%d" % s], ["oT_loc"])
            P.barrier()

        if stop_after >= 3:
            sb.reset(pa0)
            wo_bf = sb.take([NKC, 1024], BF16)
            wF_e = sb.take([4, 1024], F32)
            e_base = sb.off
            sl_we = P.slot()
            NKT = LK // 128
            kT_h = sb.take([2, LK], BF16)
            V1 = sb.take([NKT, 258], BF16)
            qT_b = [sb.take([2, 512], BF16) for _ in range(2)]
            G_b = [sb.take([4, 256], BF16) for _ in range(2)]
            PT = [sb.take([1, 1024], BF16)[:, 0, :] for _ in range(3)]
            o1 = sb.take([4, 256], F32)
            rsD = sb.take([1, 8], F32)[:, 0, :]
            ssrD = sb.take([1, 8], F32)[:, 0, :]
            tmpsD = [(sb.take([1, 8], F32)[:, 0, :], sb.take([1, 256], F32)[:, 0, :], sb.take([1, 256], BF16)[:, 0, :]) for _ in range(4)]
            st_oD = [sb.take([2, 512], BF16) for _ in range(2)]
            sl_k = P.slot()
            sl_v = P.slot()
            sl_q = [P.slot(), P.slot()]
            sl_g = [P.slot(), P.slot()]
            sl_oD = [P.slot(), P.slot()]
            cD = {"tq": 0, "pt": 0, "s": 0}
            st_by_blk = {}
            memset("dve", V1[:, :, 256:258], 1.0, [], ["V1ones"])
            for hl in range(2):
                if hl == 1:
                    for q in range(8):
                        P.dma("sp", wF_e, wout[q * 512:(q + 1) * 512, :].rearrange("(k p) n -> p k n", p=128), sl_we, [], ["wF_e"])
                        cp(("dve", "pool")[q % 2], wo_bf[:, q * 4:(q + 1) * 4, :], wF_e, ["wF_e"], ["wo_bf"])
                dma_group("sp", [(kT_h[:, i, :], kT_s[2 * hl + i]) for i in range(2)], sl_k, [], ["kT_h"])
                dma_group("sp", [(V1[:, q * 11:(q + 1) * 11, 0:256],
                                  v_s[q * 1408:(q + 1) * 1408, hl * 256:(hl + 1) * 256].rearrange("(c p) d -> p c d", p=128))
                                 for q in range(6)], sl_v, [], ["V1"])
                for qb in (d_qblocks if d_qblocks is not None else range(16)):
                    s = qb % 2
                    P.dma("sp", qT_b[s], qT_s[2 * hl:2 * hl + 2, :, qb * 512:(qb + 1) * 512].rearrange("m p t -> p m t"),
                          sl_q[s], [], ["qT_b%d" % s])
                    P.dma("sp", G_b[s], g_s[qb * 512:(qb + 1) * 512, hl * 256:(hl + 1) * 256].rearrange("(t p) d -> p t d", p=128),
                          sl_g[s], [], ["G_b%d" % s])
                    for i in range(2):
                        def Sp(j):
                            pp = (0, 3)[j % 2]
                            for h in range(2):
                                kt = 2 * j + h
                                bs = 2 * pp + h
                                mm(psf(bs), kT_h[:, i, kt * 128:(kt + 1) * 128], qT_b[s][:, i, :], start=True, stop=True,
                                   r=["kT_h", "qT_b%d" % s], w=["ps%d" % bs])
                        NPR = NKT // 2
                        Sp(0)
                        for j in range(NPR):
                            if j + 1 < NPR:
                                Sp(j + 1)
                            pp = (0, 3)[j % 2]
                            cD["pt"] += 1
                            pr_ = cD["pt"] % 3
                            act(PT[pr_], PP[pp][:], AF.Exp, ["ps%d" % (2 * pp), "ps%d" % (2 * pp + 1)], ["PT%d" % pr_], scale=ATT_SCALE)
                            for h in range(2):
                                kt = 2 * j + h
                                for sub in range(4):
                                    mm(psf(2 + sub)[:, 0:257], PT[pr_][:, h * 512 + sub * 128:h * 512 + (sub + 1) * 128], V1[:, kt, 0:257],
                                       start=(kt == 0), stop=(kt == NKT - 1), r=["PT%d" % pr_, "V1", "V1ones"], w=["ps%d" % (2 + sub)])
                        for sub in range(4):
                            pk = "ps%d" % (2 + sub)
                            acc = psf(2 + sub)
                            recip(rsD[:, sub:sub + 1], acc[:, 256:257], [pk], ["rsD"])
                            if i == 0:
                                ts("dve", o1[:, sub, :], acc[:, 0:256], rsD[:, sub:sub + 1], None, ALU.mult, None, [pk, "rsD"], ["o1"])
                            else:
                                ts("dve", rsD[:, 4 + sub:5 + sub], rsD[:, sub:sub + 1], lam_s[:, 1:2], None, ALU.mult, None, ["rsD"], ["rsD2"])
                                stt(o1[:, sub, :], acc[:, 0:256], rsD[:, 4 + sub:5 + sub], o1[:, sub, :], ALU.mult, ALU.add,
                                    [pk, "rsD2", "o1"], ["o1"])
                    for sub in range(4):
                        cD["tq"] += 1
                        out_norm_store(o1[:, sub, :], "o1", subw_r, G_b[s][:, sub, :], "G_b%d" % s,
                                       st_oD[s], "st_oD%d" % s, sub * 128, tmpsD, cD["tq"], tbanks=(2 + sub,))
                    r0 = hl * 256
                    h_st = P.dma("pool", oT_loc_t[qb].ap()[r0:r0 + 256, :].rearrange("(c p) t -> p c t", p=128),
                                 st_oD[s], sl_oD[s], ["st_oD%d" % s], ["oT_loc"])
                    st_by_blk.setdefault(qb, []).append(h_st)
            P.barrier()

        if stop_after >= 4:
            sb.reset(e_base)
            oT_b = [sb.take([NKC, 512], BF16) for _ in range(2)]
            xr_b = [sb.take([8, 512], F32) for _ in range(2)]
            sl_ob = [P.slot(), P.slot()]
            sl_xr = [P.slot(), P.slot()]
            sl_out = [P.slot(), P.slot()]
            if e_gather:
                for qb in (e_blocks if e_blocks is not None else range(16)):
                    sl_ag = P.slot()
                    sl_ag.count = 1

                    def ag_fn(e, qb=qb):
                        return e.collective_compute(
                            "AllGather", ALU.bypass, replica_groups=[[0, 1, 2, 3], [4, 5, 6, 7]],
                            ins=[oT_loc_t[qb].ap().opt()], outs=[oT_all_t[qb].ap().opt()])
                    ag = Op("pool", ag_fn, list(P.pending["pool"]), inc=(sl_ag.sem, None))
                    P.pending["pool"] = []
                    P.streams["pool"].append(ag)
                    ag_h[qb] = DmaH(sl_ag.sem, 1)
            cE = {"acc": 0}
            for tb in (e_blocks if e_blocks is not None else range(16)):
                s = tb % 2
                cols = slice(tb * 512, (tb + 1) * 512)
                if e_gather:
                    dma_group("sp", [(oT_b[s][:, q * 16:(q + 1) * 16, :],
                                      oT_all_t[tb].ap()[q * 2048:(q + 1) * 2048, :].rearrange("(k p) t -> p k t", p=128)) for q in range(2)],
                              sl_ob[s], [], ["oT_b%d" % s], after=[ag_h[tb]])
                else:
                    dma_group("sp", [(oT_b[s][:, q * 8:(q + 1) * 8, :],
                                      oT_loc_t[tb].ap()[:, :].rearrange("(k p) t -> p k t", p=128)) for q in range(4)],
                              sl_ob[s], [], ["oT_b%d" % s])
                P.dma("sp", xr_b[s], xrT[:, cols].rearrange("(c p) t -> p c t", p=128), sl_xr[s], [], ["xr_b%d" % s])
                for cc in range(8):
                    cE["acc"] += 1
                    bk = cE["acc"] % 4
                    for kc in range(NKC):
                        mm(psf(bk), wo_bf[:, kc, cc * 128:(cc + 1) * 128], oT_b[s][:, kc, :], start=(kc == 0), stop=(kc == NKC - 1),
                           r=["wo_bf", "oT_b%d" % s], w=["ps%d" % bk])
                    stt(xr_b[s][:, cc, :], psf(bk), modsb[:, 64 + cc, 0:1], xr_b[s][:, cc, :], ALU.mult, ALU.add,
                        ["ps%d" % bk, "xr_b%d" % s], ["xr_b%d" % s])
                P.dma("pool", outT[:, cols].rearrange("(c p) t -> p c t", p=128), xr_b[s], sl_out[s], ["xr_b%d" % s], ["outT"])
        P.emit(block)
    return nc


O_DQ, O_DK, O_DV, O_DG, O_RQ, O_RK, O_RV, O_RG = 0, 2048, 4096, 6144, 8192, 9216, 10240, 12288


def _win_cols(g):
    hA, hB = 2 * g, 2 * g + 1
    cols = []
    for o in (O_DQ, O_DK, O_DV, O_DG):
        for h in (hA, hB):
            cols += list(range(o + h * 256, o + (h + 1) * 256))
    for o in (O_RQ, O_RK):
        for h in (hA, hB):
            cols += list(range(o + h * 128, o + (h + 1) * 128))
    for o in (O_RV, O_RG):
        for h in (hA, hB):
            cols += list(range(o + h * 256, o + (h + 1) * 256))
    return np.array(cols)


def _mix_rows():
    rows = []
    for r in range(4):
        for h in (2 * r, 2 * r + 1):
            rows += list(range(h * 256, (h + 1) * 256))
        for h in (2 * r, 2 * r + 1):
            rows += list(range(2048 + h * 256, 2048 + (h + 1) * 256))
    return np.array(rows)


def _const_tables():
    rows = L // 64
    row, col = np.meshgrid(np.arange(rows), np.arange(64), indexing="ij")
    row = row.reshape(-1).astype(np.float32)
    col = col.reshape(-1).astype(np.float32)
    half = 64
    inv_freq = (np.float32(10000.0) ** (-np.arange(0, half, 2, dtype=np.float32) / np.float32(half))).astype(np.float32)
    ang_r = row[:, None] * inv_freq
    ang_c = col[:, None] * inv_freq
    ang = np.concatenate([ang_r, ang_r, ang_c, ang_c], axis=-1).astype(np.float32)
    cos = np.cos(ang).astype(np.float32)
    sin = np.sin(ang).astype(np.float32)
    sign = np.concatenate([-np.ones(32), np.ones(32), -np.ones(32), np.ones(32)]).astype(np.float32)
    ropeq = np.stack([cos, sin * sign], axis=1).astype(np.float32)
    j = np.arange(128, dtype=np.float32)[:, None]
    i = np.arange(128, dtype=np.float32)[None, :]
    rc_mat = np.stack([np.maximum(i - j, 0), (i >= j).astype(np.float32),
                       np.maximum(j - i, 0), (j > i).astype(np.float32)], axis=1).astype(np.float32)
    rc_row = np.stack([np.broadcast_to(i + 1, (128, 128)), np.broadcast_to(128 - i, (128, 128))], axis=1).astype(np.float32)
    p = np.arange(128, dtype=np.float32)
    rc_col = np.zeros((128, 8), np.float32)
    rc_col[:, 0] = 127 - p
    rc_col[:, 1] = p
    rc_col[:, 2] = 128
    rc_col[:, 3] = 255 - p
    rc_col[:, 4] = 128 + p
    ident = np.eye(128, dtype=np.float32).astype(ml_dtypes.bfloat16)
    return dict(ropeq=np.ascontiguousarray(ropeq), rc_mat=np.ascontiguousarray(rc_mat),
                rc_row=np.ascontiguousarray(rc_row), rc_col=rc_col, ident=ident)


def prepare_inputs(x, c, ctx, c_ctx, norm_w, ada_w, ada_b, w_in, diff_q_norm_w, diff_k_norm_w,
                   diff_lambda_q1, diff_lambda_k1, diff_lambda_q2, diff_lambda_k2, diff_subln_w,
                   ret_decay_fwd, ret_decay_bwd, ret_norm_w, w_out):
    f = lambda a: np.asarray(a, dtype=np.float32)
    x, c, ctx, c_ctx = f(x), f(c), f(ctx), f(c_ctx)
    ada_w0, ada_b0, w_in0, w_out0 = f(ada_w)[0], f(ada_b)[0], f(w_in)[0], f(w_out)[0]
    consts = _const_tables()
    mix_rows = _mix_rows()

    def pk(v):
        return np.ascontiguousarray(v.reshape(-1, 128).T)

    normw = pk(f(norm_w)[0])
    lam4 = np.concatenate([f(diff_lambda_q1)[0], f(diff_lambda_k1)[0], f(diff_lambda_q2)[0], f(diff_lambda_k2)[0]])
    maps = []
    ada_ss = ada_w0[:, 0:8192]
    for i in range(8):
        b, g = i // 4, i % 4
        gcols = slice(8192 + g * 1024, 8192 + (g + 1) * 1024)
        m = dict(consts)
        m["x"] = x[b]
        m["ctx"] = ctx[b]
        m["xrT"] = np.ascontiguousarray(x[b][:, g * 1024:(g + 1) * 1024].T)
        m["cvec"] = np.ascontiguousarray(np.stack([pk(c[b]), pk(c_ctx)], axis=-1))
        m["normw"] = normw
        m["adaw"] = np.ascontiguousarray(np.concatenate([ada_ss, ada_w0[:, gcols]], axis=1))
        m["adab"] = pk(np.concatenate([ada_b0[0:8192], ada_b0[gcols]]))
        m["win"] = np.ascontiguousarray(w_in0[:, _win_cols(g)])
        m["wout"] = np.ascontiguousarray(w_out0[mix_rows][:, g * 1024:(g + 1) * 1024])
        m["qnw"] = f(diff_q_norm_w)[0]
        m["knw"] = f(diff_k_norm_w)[0]
        m["lam4"] = lam4
        m["subw"] = f(diff_subln_w)[0]
        m["retw"] = f(ret_norm_w)[0]
        m["dec"] = np.array([f(ret_decay_fwd)[0][2 * g], f(ret_decay_fwd)[0][2 * g + 1],
                             f(ret_decay_bwd)[0][2 * g], f(ret_decay_bwd)[0][2 * g + 1]], np.float32)
        maps.append(m)
    return maps


_NC_CACHE = {}


def kernel(**inputs):
    maps = prepare_inputs(**inputs)
    if "nc" not in _NC_CACHE:
        _NC_CACHE["nc"] = build_program()
    res = run_bass_kernel_spmd(_NC_CACHE["nc"], maps, core_ids=list(range(8)))
    out = np.empty((2, L, D), np.float32)
    for i in range(8):
        b, g = i // 4, i % 4
        out[b][:, g * 1024:(g + 1) * 1024] = res.results[i]["outT"].T
    return out
```

```python
import os
from contextlib import ExitStack

import numpy as np
import ml_dtypes

import concourse.bass as bass
import concourse.mybir as mybir
from concourse.bass_utils import run_bass_kernel_spmd

F32 = mybir.dt.float32
BF16 = mybir.dt.bfloat16
AF = mybir.ActivationFunctionType
ALU = mybir.AluOpType
AX = mybir.AxisListType

D = 4096
L = 8192
LC = 256
LK = L + LC
NKC = 32
TB = 512
NBLK = L // TB
EPS = 1e-6
NCOLS = 3584
NADA = 9216
LAM_INIT = 0.8 - 0.6 * 1.0
ATT_SCALE = 128.0 ** -0.5
SB_BYTES = 212480


class Op:
    __slots__ = ("eng", "fn", "waits", "signaled", "semval", "inc")

    def __init__(self, eng, fn, waits, inc=None):
        self.eng = eng
        self.fn = fn
        self.waits = waits
        self.signaled = False
        self.semval = None
        self.inc = inc


class DmaH:
    __slots__ = ("sem", "val")

    def __init__(self, sem, val):
        self.sem = sem
        self.val = val


class Slot:
    def __init__(self, sem):
        self.sem = sem
        self.count = 0


class Plan:
    ENGS = ("pe", "act", "dve", "pool", "sp")

    def __init__(self, nc, stack):
        self.nc = nc
        self.stack = stack
        self.streams = {e: [] for e in self.ENGS}
        self.keys = {}
        self.esem = {e: stack.enter_context(nc.semaphore("s_" + e)) for e in self.ENGS}
        self.slots = []
        self.pending = {e: [] for e in self.ENGS}
        self.nslot = 0

    def slot(self):
        self.nslot += 1
        s = Slot(self.stack.enter_context(self.nc.semaphore("d%d" % self.nslot)))
        self.slots.append(s)
        return s

    def _deps(self, handle, reads, writes, after):
        waits = list(after)
        for k in reads:
            st = self.keys.setdefault(k, {"w": [], "r": [], "war": []})
            waits += st["w"]
        for k in writes:
            st = self.keys.setdefault(k, {"w": [], "r": [], "war": []})
            if st["r"] or (k in reads):
                st["war"] = st["r"] + st["w"]
                st["w"] = []
                st["r"] = []
            waits += st["war"]
        for k in reads:
            if k not in writes:
                self.keys[k]["r"].append(handle)
        for k in writes:
            self.keys[k]["w"].append(handle)
        return waits

    def op(self, eng, fn, reads=(), writes=(), after=()):
        o = Op(eng, fn, None)
        waits = self._deps(o, reads, writes, after)
        waits += self.pending[eng]
        self.pending[eng] = []
        o.waits = waits
        for w in waits:
            if isinstance(w, Op):
                w.signaled = True
        self.streams[eng].append(o)
        return o

    def dma(self, q, out, in_, slot, reads=(), writes=(), after=()):
        slot.count += 16
        h = DmaH(slot.sem, slot.count)
        o = Op(q, lambda e: e.dma_start(out=out, in_=in_), None, inc=(slot.sem, 16))
        waits = self._deps(h, reads, writes, after)
        waits += self.pending[q]
        self.pending[q] = []
        o.waits = waits
        for w in waits:
            if isinstance(w, Op):
                w.signaled = True
        self.streams[q].append(o)
        return h

    def barrier(self):
        hs = []
        for e in self.ENGS:
            if self.streams[e]:
                last = self.streams[e][-1]
                if last.inc is None:
                    last.signaled = True
                    hs.append(last)
                else:
                    for o in reversed(self.streams[e]):
                        if o.inc is None:
                            o.signaled = True
                            hs.append(o)
                            break
        for s in self.slots:
            if s.count:
                hs.append(DmaH(s.sem, s.count))
        for e in self.ENGS:
            self.pending[e] = self.pending[e] + hs
        self.keys = {}

    def emit(self, block):
        for e in self.ENGS:
            n = 0
            for o in self.streams[e]:
                if o.inc is None and o.signaled:
                    n += 1
                    o.semval = n
        esem = self.esem

        def run(e, ops):
            seen = {}
            for o in ops:
                need = {}
                for w in o.waits:
                    if isinstance(w, Op):
                        sem, val = esem[w.eng], w.semval
                    else:
                        sem, val = w.sem, w.val
                    k = id(sem)
                    if seen.get(k, 0) >= val:
                        continue
                    if k not in need or need[k][1] < val:
                        need[k] = (sem, val)
                for k, (sem, val) in need.items():
                    e.wait_ge(sem, val)
                    seen[k] = val
                ins = o.fn(e)
                if o.inc is not None:
                    if o.inc[1] is None:
                        ins.then_inc(o.inc[0])
                    else:
                        ins.then_inc(o.inc[0], o.inc[1])
                elif o.signaled:
                    ins.then_inc(esem[o.eng], 1)

        st = self.streams
        final = [DmaH(s.sem, s.count) for s in self.slots if s.count]

        @block.tensor
        def _(e):
            run(e, st["pe"])

        @block.scalar
        def _(e):
            run(e, st["act"])

        @block.vector
        def _(e):
            run(e, st["dve"])

        @block.gpsimd
        def _(e):
            run(e, st["pool"])
            for h in final:
                e.wait_ge(h.sem, h.val)

        @block.sync
        def _(e):
            run(e, st["sp"])
            for h in final:
                e.wait_ge(h.sem, h.val)


class SBAlloc:
    def __init__(self, big):
        self.big = big
        self.off = 0

    def reset(self, off):
        self.off = off

    def take(self, shape, dt):
        n = 1
        for s in shape:
            n *= s
        nbytes = n * (4 if dt == F32 else 2)
        nbytes = (nbytes + 63) // 64 * 64
        off = self.off
        self.off += nbytes
        assert self.off <= SB_BYTES, ("SBUF overflow", self.off)
        ap = self.big[:, off // 2:(off + nbytes) // 2]
        if dt == F32:
            ap = ap.bitcast(F32)
        ap = ap[:, 0:n]
        if len(shape) == 2:
            names = "a b"
            return ap.rearrange("p (a b) -> p a b", a=shape[0])
        if len(shape) == 3:
            return ap.rearrange("p (a b c) -> p a b c", a=shape[0], b=shape[1])
        return ap


def build_program(stop_after=99, debug=(), b_blocks=None, d_qblocks=None, e_blocks=None, e_gather=True):
    if b_blocks is None:
        b_blocks = [NBLK] + list(range(NBLK))
    nc = bass.Bass("TRN2", target_bir_lowering=False)

    def din(name, shape, dt=F32):
        return nc.dram_tensor(name, list(shape), dt, kind="ExternalInput").ap()

    def dscr(name, shape, dt=BF16):
        kind = "ExternalOutput" if name in debug else "Internal"
        return nc.dram_tensor(name, list(shape), dt, kind=kind).ap()

    x = din("x", [L, D])
    ctx = din("ctx", [LC, D])
    xrT = din("xrT", [1024, L])
    cvec = din("cvec", [128, NKC, 2])
    normw = din("normw", [128, NKC])
    adaw = din("adaw", [D, NADA])
    adab = din("adab", [128, 72])
    win = din("win", [D, NCOLS])
    wout = din("wout", [D, 1024])
    qnw = din("qnw", [128])
    knw = din("knw", [128])
    lam4 = din("lam4", [4 * 128])
    subw = din("subw", [256])
    retw = din("retw", [256])
    dec = din("dec", [4])
    ident_d = din("ident", [128, 128], BF16)
    ropeq = din("ropeq", [L, 2, 128])
    rc_mat = din("rc_mat", [128, 4, 128])
    rc_row = din("rc_row", [128, 2, 128])
    rc_col = din("rc_col", [128, 8])
    outT = nc.dram_tensor("outT", [1024, L], F32, kind="ExternalOutput").ap()

    win_bf = dscr("win_bf", [7, 128, NKC, 512])
    qT_s = dscr("qT_s", [4, 128, L])
    kT_s = dscr("kT_s", [4, 128, LK])
    v_s = dscr("v_s", [LK, 512])
    g_s = dscr("g_s", [L, 512])
    rqT_s = dscr("rqT_s", [2, 128, L])
    rkT_s = dscr("rkT_s", [2, 128, L])
    rk_s = dscr("rk_s", [LK, 256])
    rv_s = dscr("rv_s", [LK, 512])
    rg_s = dscr("rg_s", [L, 512])
    oT_loc_t = [nc.dram_tensor("oT_loc%d" % i, [1024, TB], BF16,
                               kind="ExternalOutput" if "oT_loc" in debug else "Internal") for i in range(NBLK)]
    oT_all_t = [nc.dram_tensor("oT_all%d" % i, [4096, TB], BF16) for i in range(NBLK)]
    dbg_mod = dscr("dbg_mod", [128, 144], F32) if "dbg_mod" in debug else None

    with ExitStack() as stack:
        big_t = stack.enter_context(nc.sbuf_tensor("big", [128, SB_BYTES // 2], BF16))
        big = big_t[:]
        PP = [stack.enter_context(nc.psum_tensor("pp%d" % i, [128, 1024], F32)) for i in range(4)]
        P = Plan(nc, stack)
        block = stack.enter_context(nc.Block())
        sb = SBAlloc(big)

        def psf(i):
            return PP[i // 2][:, (i % 2) * 512:(i % 2 + 1) * 512]

        def psb(i):
            return psf(i).bitcast(BF16)

        def act(out, in_, func, r, w, bias=None, scale=None, accum=None, after=()):
            kw = {}
            if bias is not None:
                kw["bias"] = bias
            if scale is not None:
                kw["scale"] = scale
            if accum is not None:
                kw["accum_out"] = accum
            return P.op("act", lambda e: e.activation(out=out, in_=in_, func=func, **kw), r, w, after)

        def tt(eng, out, in0, in1, op, r, w, after=()):
            return P.op(eng, lambda e: e.tensor_tensor(out=out, in0=in0, in1=in1, op=op), r, w, after)

        def ts(eng, out, in0, s1, s2, op0, op1, r, w, after=()):
            if s2 is None:
                return P.op(eng, lambda e: e.tensor_scalar(out=out, in0=in0, scalar1=s1, scalar2=None, op0=op0), r, w, after)
            return P.op(eng, lambda e: e.tensor_scalar(out=out, in0=in0, scalar1=s1, scalar2=s2, op0=op0, op1=op1), r, w, after)

        def stt(out, in0, scalar, in1, op0, op1, r, w, after=()):
            return P.op("dve", lambda e: e.scalar_tensor_tensor(out=out, in0=in0, scalar=scalar, in1=in1, op0=op0, op1=op1), r, w, after)

        def cp(eng, out, in_, r, w, after=()):
            if eng == "act":
                return P.op("act", lambda e: e.activation(out=out, in_=in_, func=AF.Copy), r, w, after)
            return P.op(eng, lambda e: e.tensor_copy(out=out, in_=in_), r, w, after)

        def recip(out, in_, r, w, after=()):
            return P.op("dve", lambda e: e.reciprocal(out=out, in_=in_), r, w, after)

        def red(out, in_, r, w, after=()):
            return P.op("dve", lambda e: e.tensor_reduce(out=out, in_=in_, axis=AX.X, op=ALU.add), r, w, after)

        def mm(out, lhsT, rhs, start, stop, r, w, after=(), skip=False):
            if skip:
                return P.op("pe", lambda e: e.matmul(out, lhsT, rhs, start=start, stop=stop, skip_group_check=True), r, w, after)
            return P.op("pe", lambda e: e.matmul(out, lhsT, rhs, start=start, stop=stop), r, w, after)

        def tr(out, in_, r, w, after=()):
            return P.op("pe", lambda e: e.transpose(out, in_, ident), r + ["ident"], w, after)

        def memset(eng, ap, val, r, w, after=()):
            return P.op(eng, lambda e: e.memset(ap, val), r, w, after)

        def rstd_of(out, ss, n, r, w):
            act(out, ss, AF.Ln, r, w, bias=EPS, scale=1.0 / n)
            act(out, out, AF.Exp, w, w, scale=-0.5)

        ident = sb.take([1, 128], BF16)[:, 0, :]
        modsb = sb.take([72, 2], F32)
        a_lat = sb.take([1, NKC], F32)[:, 0, :]
        a_ctx = sb.take([1, NKC], F32)[:, 0, :]
        normw_t = sb.take([1, NKC], F32)[:, 0, :]
        adab_t = sb.take([1, 72], F32)[:, 0, :]
        qnw_r = sb.take([1, 128], F32)[:, 0, :]
        knw_r = sb.take([1, 128], F32)[:, 0, :]
        subw_r = sb.take([1, 256], F32)[:, 0, :]
        retw_r = sb.take([1, 256], F32)[:, 0, :]
        lam_t = sb.take([4, 128], F32)
        lam_s = sb.take([1, 8], F32)[:, 0, :]
        dec_t = sb.take([1, 4], F32)[:, 0, :]
        lg_t = sb.take([1, 4], F32)[:, 0, :]
        rcm = sb.take([4, 128], F32)
        rcr = sb.take([2, 128], F32)
        rcc = sb.take([1, 8], F32)[:, 0, :]
        DT = [sb.take([1, 128], F32)[:, 0, :] for _ in range(2)]
        XIF = [sb.take([1, 128], BF16)[:, 0, :] for _ in range(2)]
        XIB = [sb.take([1, 128], BF16)[:, 0, :] for _ in range(2)]
        rsm = sb.take([2, 8], F32)
        sc_t = sb.take([NKC, 2], BF16)
        cv_t = sb.take([NKC, 2], F32)
        const_end = sb.off
        assert const_end <= 13 * 1024, const_end

        sl_c = P.slot()
        const_h = []
        for (dst, src, key) in (
            (ident, ident_d[:, :], "ident"),
            (cv_t, cvec[:, :, :], "cv"),
            (normw_t, normw[:, :], "normw"),
            (adab_t, adab[:, :], "adab"),
            (qnw_r, qnw.partition_broadcast(128), "qnw"),
            (knw_r, knw.partition_broadcast(128), "knw"),
            (subw_r, subw.partition_broadcast(128), "subw"),
            (retw_r, retw.partition_broadcast(128), "retw"),
            (lam_t, lam4.partition_broadcast(128).rearrange("p (a b) -> p a b", a=4), "lam"),
            (dec_t, dec.partition_broadcast(128), "dec"),
            (rcm, rc_mat[:, :, :], "rcm"),
            (rcr, rc_row[:, :, :], "rcr"),
            (rcc, rc_col[:, :], "rcc"),
        ):
            const_h.append(P.dma("sp", dst, src, sl_c, [], [key]))
        for h_ in const_h:
            h_.val = sl_c.count

        act(sc_t, cv_t, AF.Silu, ["cv"], ["sc"])

        pa0 = sb.off
        adaF = [sb.take([1, NADA], F32)[:, 0, :] for _ in range(2)]
        adaB = [sb.take([1, NADA], BF16)[:, 0, :] for _ in range(2)]
        wF = [sb.take([8, 512], F32) for _ in range(2)]
        wB = [sb.take([8, 512], BF16) for _ in range(2)]
        sl_ada = [P.slot(), P.slot()]
        sl_wf = [P.slot(), P.slot()]
        sl_wst = [P.slot(), P.slot()]
        pmod = psf(0)[:, 0:144]

        wpieces = [(cb, q) for cb in range(7) for q in range(4)]
        HALF = NADA // 2
        for kc in range(NKC):
            s = kc % 2
            P.dma("sp", adaF[s], adaw[kc * 128:(kc + 1) * 128, :], sl_ada[s], [], ["adaF%d" % s])
            cp("dve", adaB[s][:, 0:HALF], adaF[s][:, 0:HALF], ["adaF%d" % s], ["adaB%d" % s])
            cp("act", adaB[s][:, HALF:NADA], adaF[s][:, HALF:NADA], ["adaF%d" % s], ["adaB%d" % s])
            for j in range(72):
                mm(pmod[:, 2 * j:2 * j + 2], adaB[s][:, j * 128:(j + 1) * 128], sc_t[:, kc, :],
                   start=(kc == 0 and j == 0), stop=(kc == NKC - 1),
                   r=["adaB%d" % s, "sc"], w=["ps0"], skip=True)
            if kc < len(wpieces):
                cb, q = wpieces[kc]
                ws = kc % 2
                src = win[q * 1024:(q + 1) * 1024, cb * 512:(cb + 1) * 512].rearrange("(k p) n -> p k n", p=128)
                P.dma("sp", wF[ws], src, sl_wf[ws], [], ["wF%d" % ws])
                cp("pool", wB[ws], wF[ws], ["wF%d" % ws], ["wB%d" % ws])
                P.dma("pool", win_bf[cb, :, q * 8:(q + 1) * 8, :], wB[ws], sl_wst[ws], ["wB%d" % ws], ["win_bf"])
        tt("dve", modsb, pmod.rearrange("p (a b) -> p a b", b=2),
           adab_t.unsqueeze(2).to_broadcast([128, 72, 2]), ALU.add, ["ps0", "adab"], ["mod"])
        stt(a_lat, modsb[:, 32:64, 0], 1.0, normw_t, ALU.add, ALU.mult, ["mod", "normw"], ["a_lat"])
        stt(a_ctx, modsb[:, 32:64, 1], 1.0, normw_t, ALU.add, ALU.mult, ["mod", "normw"], ["a_ctx"])
        if dbg_mod is not None:
            P.dma("pool", dbg_mod[:, :], modsb.rearrange("p a b -> p (a b)"), P.slot(), ["mod"], [])

        junk128 = sb.take([1, 128], F32)[:, 0, :]
        tt("dve", junk128, lam_t[:, 0, :], lam_t[:, 1, :], ALU.mult, ["lam"], ["junk128"])
        red(lam_s[:, 2:3], junk128, ["junk128"], ["lam_s"])
        tt("dve", junk128, lam_t[:, 2, :], lam_t[:, 3, :], ALU.mult, ["lam", "lam_s"], ["junk128"])
        red(lam_s[:, 3:4], junk128, ["junk128"], ["lam_s"])
        act(lam_s[:, 4:6], lam_s[:, 2:4], AF.Exp, ["lam_s"], ["lam_s"])
        stt(lam_s[:, 0:1], lam_s[:, 4:5], LAM_INIT, lam_s[:, 5:6], ALU.add, ALU.subtract, ["lam_s"], ["lam_s"])
        ts("dve", lam_s[:, 1:2], lam_s[:, 0:1], -1.0, None, ALU.mult, None, ["lam_s"], ["lam_s"])
        ts("dve", subw_r, subw_r, 1.0 - LAM_INIT, None, ALU.mult, None, ["subw"], ["subw"])

        act(lg_t, dec_t, AF.Exp, ["dec"], ["lg"], scale=-float(np.log(2.0)))
        act(lg_t, lg_t, AF.Ln, ["lg"], ["lg"], bias=1.0, scale=-1.0)
        tmpm = sb.take([2, 128], F32)
        for hl in range(2):
            lf = lg_t[:, hl:hl + 1]
            lb = lg_t[:, 2 + hl:3 + hl]
            act(tmpm[:, 0, :], rcm[:, 0, :], AF.Exp, ["rcm", "lg"], ["tmpm"], scale=lf)
            act(tmpm[:, 1, :], rcm[:, 2, :], AF.Exp, ["rcm", "lg"], ["tmpm"], scale=lb)
            tt("dve", tmpm[:, 0, :], tmpm[:, 0, :], rcm[:, 1, :], ALU.mult, ["tmpm", "rcm"], ["tmpm"])
            tt("dve", tmpm[:, 1, :], tmpm[:, 1, :], rcm[:, 3, :], ALU.mult, ["tmpm", "rcm"], ["tmpm"])
            tt("dve", tmpm[:, 0, :], tmpm[:, 0, :], tmpm[:, 1, :], ALU.add, ["tmpm"], ["tmpm"])
            ts("dve", DT[hl], tmpm[:, 0, :], ATT_SCALE, None, ALU.mult, None, ["tmpm"], ["DT%d" % hl])
            act(tmpm[:, 0, :], rcr[:, 0, :], AF.Exp, ["rcr", "lg", "tmpm"], ["tmpm"], scale=lf)
            act(tmpm[:, 1, :], rcr[:, 1, :], AF.Exp, ["rcr", "lg", "tmpm"], ["tmpm"], scale=lb)
            ts("dve", XIF[hl], tmpm[:, 0, :], ATT_SCALE, None, ALU.mult, None, ["tmpm"], ["XIF%d" % hl])
            ts("dve", XIB[hl], tmpm[:, 1, :], ATT_SCALE, None, ALU.mult, None, ["tmpm"], ["XIB%d" % hl])
            act(rsm[:, hl, 0:1], rcc[:, 0:1], AF.Exp, ["rcc", "lg"], ["rsm"], scale=lf)
            act(rsm[:, hl, 1:2], rcc[:, 1:2], AF.Exp, ["rcc", "lg"], ["rsm"], scale=lb)
            act(rsm[:, hl, 2:3], rcc[:, 2:3], AF.Exp, ["rcc", "lg"], ["rsm"], scale=lf)
            act(rsm[:, hl, 3:4], rcc[:, 2:3], AF.Exp, ["rcc", "lg"], ["rsm"], scale=lb)
            act(rsm[:, hl, 4:5], rcc[:, 3:4], AF.Exp, ["rcc", "lg"], ["rsm"], scale=lf)
            act(rsm[:, hl, 5:6], rcc[:, 0:1], AF.Exp, ["rcc", "lg"], ["rsm"], scale=lf)
            act(rsm[:, hl, 6:7], rcc[:, 1:2], AF.Exp, ["rcc", "lg"], ["rsm"], scale=lb)
            act(rsm[:, hl, 7:8], rcc[:, 4:5], AF.Exp, ["rcc", "lg"], ["rsm"], scale=lb)

        P.barrier()
        if stop_after >= 1:
            sb.reset(pa0)
            xt = sb.take([1, D], F32)[:, 0, :]
            xs2 = [sb.take([1, D], BF16)[:, 0, :] for _ in range(2)]
            hT = [sb.take([NKC, TB], BF16) for _ in range(2)]
            Wb = [sb.take([NKC, 512], BF16) for _ in range(2)]
            rq_t = sb.take([4, 256], F32)
            fset = [(sb.take([4, 128], F32), sb.take([4, 128], F32), sb.take([4, 128], F32), sb.take([1, 8], F32)[:, 0, :]) for _ in range(2)]
            qr = [sb.take([4, 128], BF16) for _ in range(2)]
            ssn2 = [sb.take([1, 8], F32)[:, 0, :] for _ in range(2)]
            st_qT = sb.take([4, TB], BF16)
            st_kT = sb.take([4, TB], BF16)
            st_rT = sb.take([4, TB], BF16)
            st_v = [sb.take([1, 512], BF16)[:, 0, :]] * 2
            st_g = [sb.take([1, 512], BF16)[:, 0, :]] * 2
            st_rv = [sb.take([1, 512], BF16)[:, 0, :]] * 2
            st_rg = [sb.take([1, 512], BF16)[:, 0, :]] * 2
            sl_x = P.slot()
            sl_w = [P.slot(), P.slot()]
            sl_rope = P.slot()
            sl_rope2 = P.slot()
            sl_st = {k: P.slot() for k in ("qT", "kT", "rT", "v0", "v1", "g0", "g1", "rv0", "rv1", "rg0", "rg1", "rk0", "rk1")}
            cnt = {"w": 0, "acc": 0, "tq": 0, "ev": 0, "qr": 0, "tok": 0}
            pend = []

            def rope(src, NM, cosT, sinT, out, r, w, fi):
                f1, f2, f3, ss4 = fset[fi]
                F1, F3 = "f1_%d" % fi, "f3_%d" % fi
                f1v = f1[:, 0:NM, :]
                f3v = f3[:, 0:NM, :]
                tt("dve", f1v, src, cosT.unsqueeze(1).to_broadcast([128, NM, 128]), ALU.mult, r, [F1])
                s5 = src.rearrange("p m (h e q) -> p m h e q", h=2, e=2)
                o5 = f3v.rearrange("p m (h e q) -> p m h e q", h=2, e=2)
                sn = sinT.rearrange("p (h e q) -> p h e q", h=2, e=2)
                tt("dve", o5[:, :, :, 0, :], s5[:, :, :, 1, :],
                   sn[:, :, 0, :].unsqueeze(1).to_broadcast([128, NM, 2, 32]), ALU.mult, r, [F3])
                tt("dve", o5[:, :, :, 1, :], s5[:, :, :, 0, :],
                   sn[:, :, 1, :].unsqueeze(1).to_broadcast([128, NM, 2, 32]), ALU.mult, r, [F3])
                tt("dve", out, f1v, f3v, ALU.add, [F1, F3], w)

            def norm_p1(blk, t, sl):
                is_ctx = blk == NBLK
                src = ctx[t * 128:(t + 1) * 128, :] if is_ctx else x[blk * TB + t * 128: blk * TB + (t + 1) * 128, :]
                xs = xs2[sl]
                ssn = ssn2[sl]
                P.dma("sp", xt, src, sl_x, [], ["xt"])
                act(xs, xt, AF.Square, ["xt"], ["xs%d" % sl, "ssn%d_0" % sl], accum=ssn[:, 0:1])
                rstd_of(ssn[:, 1:2], ssn[:, 0:1], D, ["ssn%d_0" % sl], ["ssn%d_1" % sl])
                ts("dve", xs, xt, ssn[:, 1:2], None, ALU.mult, None, ["xt", "ssn%d_1" % sl], ["xs%d" % sl])

            def norm_p2(blk, t, hs, sl):
                is_ctx = blk == NBLK
                col = 1 if is_ctx else 0
                a_v = a_ctx if is_ctx else a_lat
                xs = xs2[sl]
                banks = (0, 1, 6, 0)

                def trg(g):
                    bank = banks[g]
                    for j in range(8):
                        kc = g * 8 + j
                        tr(psb(bank)[:, j * 128:(j + 1) * 128], xs[:, kc * 128:(kc + 1) * 128], ["xs%d" % sl], ["ps%d" % bank])

                def evg(g):
                    bank = banks[g]
                    for j in range(8):
                        kc = g * 8 + j
                        o_ = hT[hs][:, kc, t * 128:(t + 1) * 128]
                        i_ = psb(bank)[:, j * 128:(j + 1) * 128]
                        a_c = a_v[:, kc:kc + 1]
                        s_c = modsb[:, kc, col:col + 1]
                        if g % 2 == 0:
                            ts("dve", o_, i_, a_c, s_c, ALU.mult, ALU.add, ["ps%d" % bank], ["hT%d" % hs])
                        else:
                            act(o_, i_, AF.Identity, ["ps%d" % bank], ["hT%d" % hs], bias=s_c, scale=a_c)
                trg(0)
                trg(1)
                trg(2)
                evg(0)
                trg(3)
                evg(1)
                evg(2)
                evg(3)

            def store_tok(name, dst_rows, src_ap, key):
                P.dma("pool", dst_rows, src_ap, sl_st[name], [key], [name + "_dram"])

            def post(blk, t, cb, bank, ntile):
                is_ctx = blk == NBLK
                ps = psf(bank)
                pk = "ps%d" % bank
                krow = (L + t * 128) if is_ctx else (blk * TB + t * 128)
                cosq = rq_t[:, t, 0:128]
                sinq = rq_t[:, t, 128:256]
                cnt["f"] = cnt.get("f", 0) + 1
                fi = cnt["f"] % 2
                f1, f2, f3, ss4 = fset[fi]
                F1, F2 = "f1_%d" % fi, "f2_%d" % fi
                SA, SB_ = "ss4a_%d" % fi, "ss4b_%d" % fi
                last = t == ntile - 1
                if cb in (0, 1):
                    w_r = qnw_r if cb == 0 else knw_r
                    stg = st_qT if cb == 0 else st_kT
                    stn = "qT" if cb == 0 else "kT"
                    act(f1.rearrange("p a b -> p (a b)"), ps, AF.Square, [pk], [F1])
                    red(ss4[:, 0:4], f1, [F1], [SA])
                    rstd_of(ss4[:, 4:8], ss4[:, 0:4], 128, [SA], [SB_])
                    for m in range(4):
                        stt(f2[:, m, :], ps[:, m * 128:(m + 1) * 128], ss4[:, 4 + m:5 + m], w_r, ALU.mult, ALU.mult,
                            [pk, SB_], [F2])
                    cnt["qr"] += 1
                    q_ = qr[cnt["qr"] % 2]
                    qk = "qr%d" % (cnt["qr"] % 2)
                    if is_ctx:
                        cp("dve", q_, f2, [F2], [qk])
                    else:
                        rope(f2, 4, cosq, sinq, q_, [F2, "ropeq"], [qk], fi)
                    def pb(q_=q_, qk=qk, stg=stg, stn=stn, t=t, last=last, cb=cb, blk=blk, is_ctx=is_ctx):
                        cnt["tq"] += 1
                        tb_ = (5, 7)[cnt["tq"] % 2]
                        for m in range(4):
                            tr(psb(tb_)[:, m * 128:(m + 1) * 128], q_[:, m, :], [qk], ["ps%d" % tb_])
                        cp("act", stg[:, :, t * 128:(t + 1) * 128], psb(tb_)[:, 0:512].rearrange("p (m t) -> p m t", m=4),
                           ["ps%d" % tb_], ["st_" + stn])
                        if last:
                            if cb == 0:
                                P.dma("pool", qT_s[:, :, blk * TB:(blk + 1) * TB].rearrange("m p t -> p m t"), stg,
                                      sl_st[stn], ["st_" + stn], ["qT_dram"])
                            elif is_ctx:
                                P.dma("pool", kT_s[:, :, L:LK].rearrange("m p t -> p m t"), stg[:, :, 0:LC],
                                      sl_st[stn], ["st_" + stn], ["kT_dram"])
                            else:
                                P.dma("pool", kT_s[:, :, blk * TB:(blk + 1) * TB].rearrange("m p t -> p m t"), stg,
                                      sl_st[stn], ["st_" + stn], ["kT_dram"])
                    pend.append(pb)
                elif cb in (2, 5):
                    cnt["tok"] += 1
                    s_ = cnt["tok"] % 2
                    s_ = 0
                    nm = ("v%d" if cb == 2 else "rv%d") % s_
                    buf = (st_v if cb == 2 else st_rv)[s_]
                    dst = (v_s if cb == 2 else rv_s)[krow:krow + 128, :]
                    cp("act", buf, ps, [pk], [nm])
                    store_tok(nm, dst, buf, nm)
                elif cb in (3, 6):
                    cnt["tok"] += 1
                    s_ = cnt["tok"] % 2
                    s_ = 0
                    nm = ("g%d" if cb == 3 else "rg%d") % s_
                    buf = (st_g if cb == 3 else st_rg)[s_]
                    dst = (g_s if cb == 3 else rg_s)[krow:krow + 128, :]
                    act(buf, ps, AF.Silu, [pk], [nm])
                    store_tok(nm, dst, buf, nm)
                else:
                    cnt["qr"] += 1
                    q_ = qr[cnt["qr"] % 2]
                    qk = "qr%d" % (cnt["qr"] % 2)
                    p3 = ps.rearrange("p (m d) -> p m d", d=128)
                    if is_ctx:
                        cp("dve", q_[:, 2:4, :], p3[:, 2:4, :], [pk], [qk])
                    else:
                        rope(p3, 4, cosq, sinq, q_, [pk, "ropeq"], [qk], fi)
                    P.dma("pool", rk_s[krow:krow + 128, :].rearrange("p (m d) -> p m d", d=128), q_[:, 2:4, :], sl_st["rk%d" % (cnt["qr"] % 2)], [qk], ["rk_dram"])
                    if not is_ctx:
                        def pb(q_=q_, qk=qk, t=t, last=last, blk=blk):
                            cnt["tq"] += 1
                            tb_ = (5, 7)[cnt["tq"] % 2]
                            for m in range(4):
                                tr(psb(tb_)[:, m * 128:(m + 1) * 128], q_[:, m, :], [qk], ["ps%d" % tb_])
                            cp("act", st_rT[:, :, t * 128:(t + 1) * 128], psb(tb_)[:, 0:512].rearrange("p (m t) -> p m t", m=4),
                               ["ps%d" % tb_], ["st_rT"])
                            if last:
                                P.dma("pool", rqT_s[:, :, blk * TB:(blk + 1) * TB].rearrange("m p t -> p m t"), st_rT[:, 0:2, :],
                                      sl_st["rT"], ["st_rT"], ["rT_dram"])
                                P.dma("pool", rkT_s[:, :, blk * TB:(blk + 1) * TB].rearrange("m p t -> p m t"), st_rT[:, 2:4, :],
                                      sl_st["rT"], ["st_rT"], ["rT_dram"])
                        pend.append(pb)

            def mm_block(blk, hs, nxt):
                is_ctx = blk == NBLK
                ntile = 2 if is_ctx else 4
                cbs = (1, 2, 4, 5) if is_ctx else range(7)
                if not is_ctx:
                    P.dma("sp", rq_t, ropeq[blk * TB:(blk + 1) * TB, :, :].rearrange("(t p) c d -> p t (c d)", p=128),
                          sl_rope, [], ["ropeq"])
                nt_next = 0 if nxt is None else (2 if nxt == NBLK else 4)
                pd = [0, 0]

                def norm_step():
                    if pd[1] < pd[0]:
                        norm_p2(nxt, pd[1], 1 - hs, pd[1] % 2)
                        pd[1] += 1
                    if pd[0] < nt_next:
                        norm_p1(nxt, pd[0], pd[0] % 2)
                        pd[0] += 1
                for ci, cb in enumerate(cbs):
                    norm_step()
                    cnt["w"] += 1
                    ws = cnt["w"] % 2
                    P.dma("sp", Wb[ws], win_bf[cb], sl_w[ws], ["win_bf"], ["W%d" % ws])
                    for t in range(ntile):
                        cnt["acc"] += 1
                        bank = 2 + cnt["acc"] % 3
                        for kc in range(NKC):
                            mm(psf(bank), hT[hs][:, kc, t * 128:(t + 1) * 128], Wb[ws][:, kc, :],
                               start=(kc == 0), stop=(kc == NKC - 1), r=["hT%d" % hs, "W%d" % ws], w=["ps%d" % bank])
                        while pend:
                            pend.pop(0)()
                        post(blk, t, cb, bank, ntile)
                while pend:
                    pend.pop(0)()
                while pd[1] < nt_next:
                    norm_step()

            def ntiles(blk):
                return 2 if blk == NBLK else 4

            for t in range(ntiles(b_blocks[0])):
                norm_p1(b_blocks[0], t, t % 2)
                norm_p2(b_blocks[0], t, 0, t % 2)
            for i, blk in enumerate(b_blocks):
                nxt = b_blocks[i + 1] if i + 1 < len(b_blocks) else None
                mm_block(blk, i % 2, nxt)
            P.barrier()
        ag_h = {}

        def dma_group(q, pairs, slot, reads, writes, after=()):
            hs = [P.dma(q, o_, i_, slot, reads, writes, after) for (o_, i_) in pairs]
            for h_ in hs:
                h_.val = slot.count
            return hs

        def out_norm_store(src_ps, src_key, w_rep, gate_ap, gate_key, st_tile, st_key, col0, tmps, tq_cnt, tbanks=(6, 7), defer=None, after_b=None):
            r_ = tq_cnt % len(tmps)
            ssr, yf, yb = tmps[r_]
            act(yf, src_ps, AF.Square, [src_key], ["yf%d" % r_, "ssr0_%d" % r_], accum=ssr[:, 0:1])
            rstd_of(ssr[:, 1:2], ssr[:, 0:1], 256, ["ssr0_%d" % r_], ["ssr1_%d" % r_])
            stt(yf, src_ps, ssr[:, 1:2], w_rep, ALU.mult, ALU.mult, [src_key, "ssr1_%d" % r_], ["yf%d" % r_])
            tt("dve", yb, yf, gate_ap, ALU.mult, ["yf%d" % r_, gate_key], ["yb%d" % r_])
            tb_ = tbanks[tq_cnt % len(tbanks)]

            def part_b():
                for dc in range(2):
                    tr(psb(tb_)[:, dc * 128:(dc + 1) * 128], yb[:, dc * 128:(dc + 1) * 128], ["yb%d" % r_], ["ps%d" % tb_])
                cp("dve", st_tile[:, :, col0:col0 + 128], psb(tb_)[:, 0:256].rearrange("p (c t) -> p c t", c=2),
                   ["ps%d" % tb_], [st_key])
                if after_b is not None:
                    after_b()
            if defer is None:
                part_b()
            else:
                defer.append(part_b)

        if stop_after >= 2:
            sb.reset(pa0)
            rk_a = sb.take([64, 128], BF16)
            rv_a = sb.take([64, 256], BF16)
            rqT_a = sb.take([64, 128], BF16)
            rkT_a = sb.take([64, 128], BF16)
            RbS = sb.take([64, 256], BF16)
            kz = sb.take([64, 128], BF16)
            ckv = sb.take([2, 384], BF16)
            ckz = sb.take([2, 128], BF16)
            Rf = sb.take([1, 256], F32)[:, 0, :]
            Rb = sb.take([1, 256], F32)[:, 0, :]
            Rf_bf = sb.take([1, 256], BF16)[:, 0, :]
            rg_b = [sb.take([4, 256], BF16) for _ in range(2)]
            qx = [sb.take([2, 512], BF16) for _ in range(2)]
            ATb = [sb.take([1, 128], BF16)[:, 0, :] for _ in range(2)]
            tmpsC = [(sb.take([1, 8], F32)[:, 0, :], sb.take([1, 256], F32)[:, 0, :], sb.take([1, 256], BF16)[:, 0, :]) for _ in range(4)]
            st_oC = [sb.take([2, 512], BF16) for _ in range(2)]
            ssrC = sb.take([1, 8], F32)[:, 0, :]
            sl_c1 = [P.slot() for _ in range(5)]
            sl_rg = [P.slot(), P.slot()]
            sl_oC = [P.slot(), P.slot()]
            cC = {"kv": 0, "s": 0, "po": 0, "tq": 0, "at": 0}
            pendC = []
            for hl in range(2):
                kcol = slice(hl * 128, (hl + 1) * 128)
                vcol = slice(hl * 256, (hl + 1) * 256)
                dma_group("sp", [(rk_a[:, q * 16:(q + 1) * 16, :],
                                  rk_s[q * 2048:(q + 1) * 2048, kcol].rearrange("(c p) d -> p c d", p=128)) for q in range(4)],
                          sl_c1[0], [], ["rk_a"])
                dma_group("sp", [(rv_a[:, q * 16:(q + 1) * 16, :],
                                  rv_s[q * 2048:(q + 1) * 2048, vcol].rearrange("(c p) d -> p c d", p=128)) for q in range(4)],
                          sl_c1[1], [], ["rv_a"])
                P.dma("sp", rqT_a.rearrange("p c t -> p (c t)"), rqT_s[hl], sl_c1[2], [], ["rqT_a"])
                P.dma("sp", rkT_a.rearrange("p c t -> p (c t)"), rkT_s[hl], sl_c1[3], [], ["rkT_a"])
                dma_group("sp", [(ckv[:, :, 0:128], rk_s[L:LK, kcol].rearrange("(c p) d -> p c d", p=128)),
                                 (ckv[:, :, 128:384], rv_s[L:LK, vcol].rearrange("(c p) d -> p c d", p=128))],
                          sl_c1[4], [], ["ckv"])
                for (dirn, Rst, rkey, w0) in (("f", Rf, "Rf", 4), ("b", Rb, "Rb", 6)):
                    for t in range(2):
                        ts("dve", ckz[:, t, :], ckv[:, t, 0:128], rsm[:, hl, w0 + t:w0 + t + 1], None, ALU.mult, None,
                           ["ckv"], ["ckz"])
                    cC["kv"] += 1
                    bk = 4 + cC["kv"] % 2
                    for t in range(2):
                        mm(psf(bk)[:, 0:256], ckz[:, t, :], ckv[:, t, 128:384], start=(t == 0), stop=(t == 1),
                           r=["ckz", "ckv"], w=["ps%d" % bk])
                    cp("dve", Rst, psf(bk)[:, 0:256], ["ps%d" % bk], [rkey])
                ts("dve", kz, rk_a, rsm[:, hl, 1:2], None, ALU.mult, None, ["rk_a"], ["kz"])
                for c in range(63, -1, -1):
                    cp("act", RbS[:, c, :], Rb, ["Rb"], ["RbS"])
                    cC["kv"] += 1
                    bk = 4 + cC["kv"] % 2
                    mm(psf(bk)[:, 0:256], kz[:, c, :], rv_a[:, c, :], start=True, stop=True, r=["kz", "rv_a"], w=["ps%d" % bk])
                    stt(Rb, Rb, rsm[:, hl, 3:4], psf(bk)[:, 0:256], ALU.mult, ALU.add, ["Rb", "ps%d" % bk], ["Rb"])
                ts("dve", kz, rk_a, rsm[:, hl, 0:1], None, ALU.mult, None, ["rk_a"], ["kz"])
                for gi in range(16):
                    s = gi % 2
                    P.dma("sp", rg_b[s], rg_s[gi * 512:(gi + 1) * 512, vcol].rearrange("(t p) d -> p t d", p=128),
                          sl_rg[s], [], ["rg%d" % s])
                    q4 = rqT_a[:, 4 * gi:4 * gi + 4, :]
                    tt("dve", qx[s][:, 0, :].rearrange("p (c t) -> p c t", c=4), q4,
                       XIF[hl].unsqueeze(1).to_broadcast([128, 4, 128]), ALU.mult, ["rqT_a"], ["qx%d" % s])
                    tt("dve", qx[s][:, 1, :].rearrange("p (c t) -> p c t", c=4), q4,
                       XIB[hl].unsqueeze(1).to_broadcast([128, 4, 128]), ALU.mult, ["rqT_a"], ["qx%d" % s])
                    for ci in range(4):
                        c = 4 * gi + ci
                        cC["s"] += 1
                        bs = cC["s"] % 2
                        mm(psf(bs)[:, 0:128], rkT_a[:, c, :], rqT_a[:, c, :], start=True, stop=True,
                           r=["rkT_a", "rqT_a"], w=["ps%d" % bs])
                        cC["at"] += 1
                        at = ATb[cC["at"] % 2]
                        atk = "AT%d" % (cC["at"] % 2)
                        tt("dve", at, psf(bs)[:, 0:128], DT[hl], ALU.mult, ["ps%d" % bs], [atk])
                        cp("act", Rf_bf, Rf, ["Rf"], ["Rf_bf"])
                        cC["po"] += 1
                        bp = 2 + cC["po"] % 2
                        pk = "ps%d" % bp
                        mm(psf(bp)[:, 0:256], at, rv_a[:, c, :], start=True, stop=False, r=[atk, "rv_a"], w=[pk])
                        mm(psf(bp)[:, 0:256], qx[s][:, 0, ci * 128:(ci + 1) * 128], Rf_bf, start=False, stop=False,
                           r=["qx%d" % s, "Rf_bf"], w=[pk])
                        mm(psf(bp)[:, 0:256], qx[s][:, 1, ci * 128:(ci + 1) * 128], RbS[:, c, :], start=False, stop=True,
                           r=["qx%d" % s, "RbS"], w=[pk])
                        cC["kv"] += 1
                        bk = 4 + cC["kv"] % 2
                        mm(psf(bk)[:, 0:256], kz[:, c, :], rv_a[:, c, :], start=True, stop=True, r=["kz", "rv_a"], w=["ps%d" % bk])
                        stt(Rf, Rf, rsm[:, hl, 2:3], psf(bk)[:, 0:256], ALU.mult, ALU.add, ["Rf", "ps%d" % bk], ["Rf"])
                        cC["tq"] += 1
                        while pendC:
                            pendC.pop(0)()
                        r0 = 512 + hl * 256

                        def store_c(gi=gi, s=s, r0=r0):
                            P.dma("pool", oT_loc_t[gi].ap()[r0:r0 + 256, :].rearrange("(c p) t -> p c t", p=128),
                                  st_oC[s], sl_oC[s], ["st_oC%d" % s], ["oT_loc"])
                        out_norm_store(psf(bp)[:, 0:256], pk, retw_r, rg_b[s][:, ci, :], "rg%d" % s,
                                       st_oC[s], "st_oC%d" % s, ci * 128, tmpsC, cC["tq"],
                                       defer=pendC, after_b=(store_c if ci == 3 else None))
                while pendC:
                    pendC.pop(0)()
            P.barrier()

        if stop_after >= 3:
            sb.reset(pa0)
            wo_bf = sb.take([NKC, 1024], BF16)
            wF_e = sb.take([4, 1024], F32)
            e_base = sb.off
            sl_we = P.slot()
            NKT = LK // 128
            kT_h = sb.take([2, LK], BF16)
            V1 = sb.take([NKT, 258], BF16)
            qT_b = [sb.take([2, 512], BF16) for _ in range(2)]
            G_b = [sb.take([4, 256], BF16) for _ in range(2)]
            PT = [sb.take([1, 1024], BF16)[:, 0, :] for _ in range(3)]
            o1 = sb.take([4, 256], F32)
            rsD = sb.take([1, 8], F32)[:, 0, :]
            ssrD = sb.take([1, 8], F32)[:, 0, :]
            tmpsD = [(sb.take([1, 8], F32)[:, 0, :], sb.take([1, 256], F32)[:, 0, :], sb.take([1, 256], BF16)[:, 0, :]) for _ in range(4)]
            st_oD = [sb.take([2, 512], BF16) for _ in range(2)]
            sl_k = P.slot()
            sl_v = P.slot()
            sl_q = [P.slot(), P.slot()]
            sl_g = [P.slot(), P.slot()]
            sl_oD = [P.slot(), P.slot()]
            cD = {"tq": 0, "pt": 0, "s": 0}
            st_by_blk = {}
            memset("dve", V1[:, :, 256:258], 1.0, [], ["V1ones"])
            for hl in range(2):
                if hl == 1:
                    for q in range(8):
                        P.dma("sp", wF_e, wout[q * 512:(q + 1) * 512, :].rearrange("(k p) n -> p k n", p=128), sl_we, [], ["wF_e"])
                        cp(("dve", "pool")[q % 2], wo_bf[:, q * 4:(q + 1) * 4, :], wF_e, ["wF_e"], ["wo_bf"])
                dma_group("sp", [(kT_h[:, i, :], kT_s[2 * hl + i]) for i in range(2)], sl_k, [], ["kT_h"])
                dma_group("sp", [(V1[:, q * 11:(q + 1) * 11, 0:256],
                                  v_s[q * 1408:(q + 1) * 1408, hl * 256:(hl + 1) * 256].rearrange("(c p) d -> p c d", p=128))
                                 for q in range(6)], sl_v, [], ["V1"])
                for qb in (d_qblocks if d_qblocks is not None else range(16)):
                    s = qb % 2
                    P.dma("sp", qT_b[s], qT_s[2 * hl:2 * hl + 2, :, qb * 512:(qb + 1) * 512].rearrange("m p t -> p m t"),
                          sl_q[s], [], ["qT_b%d" % s])
                    P.dma("sp", G_b[s], g_s[qb * 512:(qb + 1) * 512, hl * 256:(hl + 1) * 256].rearrange("(t p) d -> p t d", p=128),
                          sl_g[s], [], ["G_b%d" % s])
                    for i in range(2):
                        def Sp(j):
                            pp = (0, 3)[j % 2]
                            for h in range(2):
                                kt = 2 * j + h
                                bs = 2 * pp + h
                                mm(psf(bs), kT_h[:, i, kt * 128:(kt + 1) * 128], qT_b[s][:, i, :], start=True, stop=True,
                                   r=["kT_h", "qT_b%d" % s], w=["ps%d" % bs])
                        NPR = NKT // 2
                        Sp(0)
                        for j in range(NPR):
                            if j + 1 < NPR:
                                Sp(j + 1)
                            pp = (0, 3)[j % 2]
                            cD["pt"] += 1
                            pr_ = cD["pt"] % 3
                            act(PT[pr_], PP[pp][:], AF.Exp, ["ps%d" % (2 * pp), "ps%d" % (2 * pp + 1)], ["PT%d" % pr_], scale=ATT_SCALE)
                            for h in range(2):
                                kt = 2 * j + h
                                for sub in range(4):
                                    mm(psf(2 + sub)[:, 0:257], PT[pr_][:, h * 512 + sub * 128:h * 512 + (sub + 1) * 128], V1[:, kt, 0:257],
                                       start=(kt == 0), stop=(kt == NKT - 1), r=["PT%d" % pr_, "V1", "V1ones"], w=["ps%d" % (2 + sub)])
                        for sub in range(4):
                            pk = "ps%d" % (2 + sub)
                            acc = psf(2 + sub)
                            recip(rsD[:, sub:sub + 1], acc[:, 256:257], [pk], ["rsD"])
                            if i == 0:
                                ts("dve", o1[:, sub, :], acc[:, 0:256], rsD[:, sub:sub + 1], None, ALU.mult, None, [pk, "rsD"], ["o1"])
                            else:
                                ts("dve", rsD[:, 4 + sub:5 + sub], rsD[:, sub:sub + 1], lam_s[:, 1:2], None, ALU.mult, None, ["rsD"], ["rsD2"])
                                stt(o1[:, sub, :], acc[:, 0:256], rsD[:, 4 + sub:5 + sub], o1[:, sub, :], ALU.mult, ALU.add,
                                    [pk, "rsD2", "o1"], ["o1"])
                    for sub in range(4):
                        cD["tq"] += 1
                        out_norm_store(o1[:, sub, :], "o1", subw_r, G_b[s][:, sub, :], "G_b%d" % s,
                                       st_oD[s], "st_oD%d" % s, sub * 128, tmpsD, cD["tq"], tbanks=(2 + sub,))
                    r0 = hl * 256
                    h_st = P.dma("pool", oT_loc_t[qb].ap()[r0:r0 + 256, :].rearrange("(c p) t -> p c t", p=128),
                                 st_oD[s], sl_oD[s], ["st_oD%d" % s], ["oT_loc"])
                    st_by_blk.setdefault(qb, []).append(h_st)
            P.barrier()

        if stop_after >= 4:
            sb.reset(e_base)
            oT_b = [sb.take([NKC, 512], BF16) for _ in range(2)]
            xr_b = [sb.take([8, 512], F32) for _ in range(2)]
            sl_ob = [P.slot(), P.slot()]
            sl_xr = [P.slot(), P.slot()]
            sl_out = [P.slot(), P.slot()]
            if e_gather:
                for qb in (e_blocks if e_blocks is not None else range(16)):
                    sl_ag = P.slot()
                    sl_ag.count = 1

                    def ag_fn(e, qb=qb):
                        return e.collective_compute(
                            "AllGather", ALU.bypass, replica_groups=[[0, 1, 2, 3], [4, 5, 6, 7]],
                            ins=[oT_loc_t[qb].ap().opt()], outs=[oT_all_t[qb].ap().opt()])
                    ag = Op("pool", ag_fn, list(P.pending["pool"]), inc=(sl_ag.sem, None))
                    P.pending["pool"] = []
                    P.streams["pool"].append(ag)
                    ag_h[qb] = DmaH(sl_ag.sem, 1)
            cE = {"acc": 0}
            for tb in (e_blocks if e_blocks is not None else range(16)):
                s = tb % 2
                cols = slice(tb * 512, (tb + 1) * 512)
                if e_gather:
                    dma_group("sp", [(oT_b[s][:, q * 16:(q + 1) * 16, :],
                                      oT_all_t[tb].ap()[q * 2048:(q + 1) * 2048, :].rearrange("(k p) t -> p k t", p=128)) for q in range(2)],
                              sl_ob[s], [], ["oT_b%d" % s], after=[ag_h[tb]])
                else:
                    dma_group("sp", [(oT_b[s][:, q * 8:(q + 1) * 8, :],
                                      oT_loc_t[tb].ap()[:, :].rearrange("(k p) t -> p k t", p=128)) for q in range(4)],
                              sl_ob[s], [], ["oT_b%d" % s])
                P.dma("sp", xr_b[s], xrT[:, cols].rearrange("(c p) t -> p c t", p=128), sl_xr[s], [], ["xr_b%d" % s])
                for cc in range(8):
                    cE["acc"] += 1
                    bk = cE["acc"] % 4
                    for kc in range(NKC):
                        mm(psf(bk), wo_bf[:, kc, cc * 128:(cc + 1) * 128], oT_b[s][:, kc, :], start=(kc == 0), stop=(kc == NKC - 1),
                           r=["wo_bf", "oT_b%d" % s], w=["ps%d" % bk])
                    stt(xr_b[s][:, cc, :], psf(bk), modsb[:, 64 + cc, 0:1], xr_b[s][:, cc, :], ALU.mult, ALU.add,
                        ["ps%d" % bk, "xr_b%d" % s], ["xr_b%d" % s])
                P.dma("pool", outT[:, cols].rearrange("(c p) t -> p c t", p=128), xr_b[s], sl_out[s], ["xr_b%d" % s], ["outT"])
        P.emit(block)
    return nc


O_DQ, O_DK, O_DV, O_DG, O_RQ, O_RK, O_RV, O_RG = 0, 2048, 4096, 6144, 8192, 9216, 10240, 12288


def _win_cols(g):
    hA, hB = 2 * g, 2 * g + 1
    cols = []
    for o in (O_DQ, O_DK, O_DV, O_DG):
        for h in (hA, hB):
            cols += list(range(o + h * 256, o + (h + 1) * 256))
    for o in (O_RQ, O_RK):
        for h in (hA, hB):
            cols += list(range(o + h * 128, o + (h + 1) * 128))
    for o in (O_RV, O_RG):
        for h in (hA, hB):
            cols += list(range(o + h * 256, o + (h + 1) * 256))
    return np.array(cols)


def _mix_rows():
    rows = []
    for r in range(4):
        for h in (2 * r, 2 * r + 1):
            rows += list(range(h * 256, (h + 1) * 256))
        for h in (2 * r, 2 * r + 1):
            rows += list(range(2048 + h * 256, 2048 + (h + 1) * 256))
    return np.array(rows)


def _const_tables():
    rows = L // 64
    row, col = np.meshgrid(np.arange(rows), np.arange(64), indexing="ij")
    row = row.reshape(-1).astype(np.float32)
    col = col.reshape(-1).astype(np.float32)
    half = 64
    inv_freq = (np.float32(10000.0) ** (-np.arange(0, half, 2, dtype=np.float32) / np.float32(half))).astype(np.float32)
    ang_r = row[:, None] * inv_freq
    ang_c = col[:, None] * inv_freq
    ang = np.concatenate([ang_r, ang_r, ang_c, ang_c], axis=-1).astype(np.float32)
    cos = np.cos(ang).astype(np.float32)
    sin = np.sin(ang).astype(np.float32)
    sign = np.concatenate([-np.ones(32), np.ones(32), -np.ones(32), np.ones(32)]).astype(np.float32)
    ropeq = np.stack([cos, sin * sign], axis=1).astype(np.float32)
    j = np.arange(128, dtype=np.float32)[:, None]
    i = np.arange(128, dtype=np.float32)[None, :]
    rc_mat = np.stack([np.maximum(i - j, 0), (i >= j).astype(np.float32),
                       np.maximum(j - i, 0), (j > i).astype(np.float32)], axis=1).astype(np.float32)
    rc_row = np.stack([np.broadcast_to(i + 1, (128, 128)), np.broadcast_to(128 - i, (128, 128))], axis=1).astype(np.float32)
    p = np.arange(128, dtype=np.float32)
    rc_col = np.zeros((128, 8), np.float32)
    rc_col[:, 0] = 127 - p
    rc_col[:, 1] = p
    rc_col[:, 2] = 128
    rc_col[:, 3] = 255 - p
    rc_col[:, 4] = 128 + p
    ident = np.eye(128, dtype=np.float32).astype(ml_dtypes.bfloat16)
    return dict(ropeq=np.ascontiguousarray(ropeq), rc_mat=np.ascontiguousarray(rc_mat),
                rc_row=np.ascontiguousarray(rc_row), rc_col=rc_col, ident=ident)


def prepare_inputs(x, c, ctx, c_ctx, norm_w, ada_w, ada_b, w_in, diff_q_norm_w, diff_k_norm_w,
                   diff_lambda_q1, diff_lambda_k1, diff_lambda_q2, diff_lambda_k2, diff_subln_w,
                   ret_decay_fwd, ret_decay_bwd, ret_norm_w, w_out):
    f = lambda a: np.asarray(a, dtype=np.float32)
    x, c, ctx, c_ctx = f(x), f(c), f(ctx), f(c_ctx)
    ada_w0, ada_b0, w_in0, w_out0 = f(ada_w)[0], f(ada_b)[0], f(w_in)[0], f(w_out)[0]
    consts = _const_tables()
    mix_rows = _mix_rows()

    def pk(v):
        return np.ascontiguousarray(v.reshape(-1, 128).T)

    normw = pk(f(norm_w)[0])
    lam4 = np.concatenate([f(diff_lambda_q1)[0], f(diff_lambda_k1)[0], f(diff_lambda_q2)[0], f(diff_lambda_k2)[0]])
    maps = []
    ada_ss = ada_w0[:, 0:8192]
    for i in range(8):
        b, g = i // 4, i % 4
        gcols = slice(8192 + g * 1024, 8192 + (g + 1) * 1024)
        m = dict(consts)
        m["x"] = x[b]
        m["ctx"] = ctx[b]
        m["xrT"] = np.ascontiguousarray(x[b][:, g * 1024:(g + 1) * 1024].T)
        m["cvec"] = np.ascontiguousarray(np.stack([pk(c[b]), pk(c_ctx)], axis=-1))
        m["normw"] = normw
        m["adaw"] = np.ascontiguousarray(np.concatenate([ada_ss, ada_w0[:, gcols]], axis=1))
        m["adab"] = pk(np.concatenate([ada_b0[0:8192], ada_b0[gcols]]))
        m["win"] = np.ascontiguousarray(w_in0[:, _win_cols(g)])
        m["wout"] = np.ascontiguousarray(w_out0[mix_rows][:, g * 1024:(g + 1) * 1024])
        m["qnw"] = f(diff_q_norm_w)[0]
        m["knw"] = f(diff_k_norm_w)[0]
        m["lam4"] = lam4
        m["subw"] = f(diff_subln_w)[0]
        m["retw"] = f(ret_norm_w)[0]
        m["dec"] = np.array([f(ret_decay_fwd)[0][2 * g], f(ret_decay_fwd)[0][2 * g + 1],
                             f(ret_decay_bwd)[0][2 * g], f(ret_decay_bwd)[0][2 * g + 1]], np.float32)
        maps.append(m)
    return maps


_NC_CACHE = {}


def kernel(**inputs):
    maps = prepare_inputs(**inputs)
    if "nc" not in _NC_CACHE:
        _NC_CACHE["nc"] = build_program()
    res = run_bass_kernel_spmd(_NC_CACHE["nc"], maps, core_ids=list(range(8)))
    out = np.empty((2, L, D), np.float32)
    for i in range(8):
        b, g = i // 4, i % 4
        out[b][:, g * 1024:(g + 1) * 1024] = res.results[i]["outT"].T
    return out
```
